# Optimizing a Trainium2 kernel written in Bass

```python
import functools
import jax, jax.numpy as jnp
from jax import lax
import numpy as np

D_MODEL = 1024
BATCH = 1
SEQ = 16384
DEPTH = 1
DEC_BATCH = 128
DEC_SEQ = 8
PAST_LEN = 16384
PAGE_SIZE = 128

N_HEADS = 16
N_KV_HEADS = 4
HEAD_DIM = D_MODEL // N_HEADS
GROUP = N_HEADS // N_KV_HEADS
ATTN_WIDTH = N_HEADS * HEAD_DIM
KV_WIDTH = N_KV_HEADS * HEAD_DIM
WINDOW = 128
BLOCK = 128
ROPE_THETA = 10000.0
CONV_WIDTH = D_MODEL
CONV_K = 3
N_META = 16
EPS = 1e-6
NEG_INF = -1e30
IN_SIZES = (ATTN_WIDTH, KV_WIDTH, KV_WIDTH, ATTN_WIDTH,
            CONV_WIDTH, CONV_WIDTH, CONV_WIDTH, CONV_WIDTH, D_MODEL, D_MODEL)
IN_WIDTH = sum(IN_SIZES)

kernel_name = 'hybrid_swa_sink_shortconv_gated_merge_step'


def _split_points():
    return [int(s) for s in np.cumsum(IN_SIZES)[:-1]]


def _rms_norm(x, w):
    xf = x.astype(jnp.float32)
    r = lax.rsqrt(jnp.mean(xf * xf, axis=-1, keepdims=True) + EPS)
    return (xf * r).astype(x.dtype) * w.astype(x.dtype)


def _rope(x, pos):
    half = HEAD_DIM // 2
    inv = jnp.power(ROPE_THETA, -jnp.arange(half, dtype=jnp.float32) * (2.0 / HEAD_DIM))
    ang = pos.astype(jnp.float32)[:, None] * inv[None, :]
    cos = jnp.cos(ang)[:, None, :]
    sin = jnp.sin(ang)[:, None, :]
    xf = x.astype(jnp.float32)
    x1, x2 = xf[..., :half], xf[..., half:]
    return jnp.concatenate([x1 * cos - x2 * sin, x2 * cos + x1 * sin], axis=-1).astype(x.dtype)


def _sink_softmax(s, sinks):
    sk = sinks.astype(jnp.float32).reshape(N_KV_HEADS, GROUP)[:, :, None, None]
    m = jnp.maximum(jnp.max(s, axis=-1, keepdims=True), sk)
    p = jnp.exp(s - m)
    return p / (jnp.sum(p, axis=-1, keepdims=True) + jnp.exp(sk - m))


def _window_attn_prompt(q, k, v, sinks, pos):
    n, L = q.shape[:2]
    nb = L // BLOCK
    qb = q.reshape(n, nb, BLOCK, N_KV_HEADS, GROUP, HEAD_DIM)
    kb = k.reshape(n, nb, BLOCK, N_KV_HEADS, HEAD_DIM)
    vb = v.reshape(n, nb, BLOCK, N_KV_HEADS, HEAD_DIM)
    shift = ((0, 0), (1, 0), (0, 0), (0, 0), (0, 0))
    kk = jnp.concatenate([jnp.pad(kb[:, :-1], shift), kb], axis=2)
    vv = jnp.concatenate([jnp.pad(vb[:, :-1], shift), vb], axis=2)
    pb = pos.reshape(nb, BLOCK)
    kpos = jnp.concatenate([pb - BLOCK, pb], axis=1)
    diff = pb[:, :, None] - kpos[:, None, :]
    mask = (diff >= 0) & (diff <= WINDOW) & (kpos[:, None, :] >= 0)
    s = jnp.einsum('bnqkgd,bnskd->bnkgqs', qb, kk).astype(jnp.float32) * (HEAD_DIM ** -0.5)
    s = jnp.where(mask[None, :, None, None], s, NEG_INF)
    p = _sink_softmax(s, sinks).astype(v.dtype)
    o = jnp.einsum('bnkgqs,bnskd->bnqkgd', p, vv)
    return o.reshape(n, L, ATTN_WIDTH)


def _window_attn_sample(q, k, v, sinks, cache_k, cache_v):
    n, T = q.shape[:2]
    kk = jnp.concatenate([cache_k.astype(k.dtype), k], axis=1)
    vv = jnp.concatenate([cache_v.astype(v.dtype), v], axis=1)
    qpos = PAST_LEN + jnp.arange(T)
    kpos = PAST_LEN - WINDOW + jnp.arange(WINDOW + T)
    diff = qpos[:, None] - kpos[None, :]
    mask = (diff >= 0) & (diff <= WINDOW)
    qg = q.reshape(n, T, N_KV_HEADS, GROUP, HEAD_DIM)
    s = jnp.einsum('btkgd,bskd->bkgts', qg, kk).astype(jnp.float32) * (HEAD_DIM ** -0.5)
    s = jnp.where(mask[None, None, None], s, NEG_INF)
    p = _sink_softmax(s, sinks).astype(v.dtype)
    o = jnp.einsum('bkgts,bskd->btkgd', p, vv)
    return o.reshape(n, T, ATTN_WIDTH)


def _causal_conv(u_ext, w):
    T = u_ext.shape[1] - (CONV_K - 1)
    w = w.astype(u_ext.dtype)
    out = w[0] * u_ext[:, 0:T]
    for i in range(1, CONV_K):
        out = out + w[i] * u_ext[:, i:i + T]
    return out


def _layer(h, pos, valid, conv_prefix, attend, norm_w, w_in, q_norm_w, k_norm_w, sinks,
           conv_w, w_proj_a, w_proj_b, w_out):
    n, t, _ = h.shape
    xn = _rms_norm(h, norm_w)
    p = jnp.einsum('ntd,de->nte', xn, w_in)
    q, k, v, z_a, b, c, hc, z_b, g_a, g_b = jnp.split(p, _split_points(), axis=-1)
    q = _rope(_rms_norm(q.reshape(n, t, N_HEADS, HEAD_DIM), q_norm_w), pos)
    k = _rope(_rms_norm(k.reshape(n, t, N_KV_HEADS, HEAD_DIM), k_norm_w), pos)
    v = v.reshape(n, t, N_KV_HEADS, HEAD_DIM)
    o_a = attend(q, k, v, sinks)
    u = jnp.where(valid[None, :, None], c * hc, 0)
    u_ext = jnp.concatenate([conv_prefix.astype(u.dtype), u], axis=1)
    o_b = b * _causal_conv(u_ext, conv_w)
    br_a = jnp.einsum('nte,ed->ntd', o_a * jax.nn.silu(z_a), w_proj_a)
    br_b = jnp.einsum('nte,ed->ntd', o_b * jax.nn.silu(z_b), w_proj_b)
    mixed = jax.nn.sigmoid(g_a) * br_a + jax.nn.sigmoid(g_b) * br_b
    return h + jnp.einsum('ntd,de->nte', mixed, w_out), k, v, u_ext


def setup_inputs(seed: int = 0) -> dict:
    key = jax.random.key(seed)
    ks = jax.random.split(key, 16)
    f32 = jnp.float32
    nrm = lambda k, shape, scale=1.0: jax.random.normal(k, shape, f32) * scale
    return {
        'x_prompt': nrm(ks[0], (BATCH, SEQ, D_MODEL)),
        'x_sample': nrm(ks[1], (DEC_BATCH, DEC_SEQ, D_MODEL)),
        'cache_k': nrm(ks[2], (DEPTH, DEC_BATCH, WINDOW, N_KV_HEADS, HEAD_DIM)),
        'cache_v': nrm(ks[3], (DEPTH, DEC_BATCH, WINDOW, N_KV_HEADS, HEAD_DIM)),
        'state_conv': nrm(ks[4], (DEPTH, DEC_BATCH, CONV_K - 1, CONV_WIDTH)),
        'meta_tokens': nrm(ks[5], (N_META, D_MODEL)),
        'norm_w': 1.0 + nrm(ks[6], (DEPTH, D_MODEL), 0.02),
        'w_in': nrm(ks[7], (DEPTH, D_MODEL, IN_WIDTH), D_MODEL ** -0.5),
        'q_norm_w': 1.0 + nrm(ks[8], (DEPTH, HEAD_DIM), 0.02),
        'k_norm_w': 1.0 + nrm(ks[9], (DEPTH, HEAD_DIM), 0.02),
        'sinks': nrm(ks[10], (DEPTH, N_HEADS)),
        'conv_w': nrm(ks[11], (DEPTH, CONV_K, CONV_WIDTH), CONV_K ** -0.5),
        'w_proj_a': nrm(ks[12], (DEPTH, ATTN_WIDTH, D_MODEL), ATTN_WIDTH ** -0.5),
        'w_proj_b': nrm(ks[13], (DEPTH, CONV_WIDTH, D_MODEL), CONV_WIDTH ** -0.5),
        'w_out': nrm(ks[14], (DEPTH, D_MODEL, D_MODEL), D_MODEL ** -0.5),
    }


def reference(x_prompt, x_sample, cache_k, cache_v, state_conv, meta_tokens, norm_w, w_in,
              q_norm_w, k_norm_w, sinks, conv_w, w_proj_a, w_proj_b, w_out):
    nb_, _, d = x_prompt.shape
    lead = BLOCK - N_META
    meta = jnp.broadcast_to(meta_tokens.astype(x_prompt.dtype)[None], (nb_, N_META, d))
    h = jnp.concatenate([jnp.zeros((nb_, lead, d), x_prompt.dtype), meta, x_prompt], axis=1)
    pos_p = jnp.arange(h.shape[1]) - lead
    valid_p = pos_p >= 0
    attend_p = functools.partial(_window_attn_prompt, pos=pos_p)
    conv0 = jnp.zeros((nb_, CONV_K - 1, CONV_WIDTH), x_prompt.dtype)
    nk_p, nv_p, nc_p = [], [], []
    for l in range(DEPTH):
        h, k, v, u_ext = _layer(h, pos_p, valid_p, conv0, attend_p, norm_w[l], w_in[l],
                                q_norm_w[l], k_norm_w[l], sinks[l], conv_w[l],
                                w_proj_a[l], w_proj_b[l], w_out[l])
        nk_p.append(k[:, -WINDOW:])
        nv_p.append(v[:, -WINDOW:])
        nc_p.append(u_ext[:, -(CONV_K - 1):])
    y_prompt = h[:, BLOCK:]

    hs = x_sample
    t = x_sample.shape[1]
    pos_s = PAST_LEN + jnp.arange(t)
    valid_s = jnp.ones((t,), dtype=bool)
    nk_s, nv_s, nc_s = [], [], []
    for l in range(DEPTH):
        attend_s = functools.partial(_window_attn_sample, cache_k=cache_k[l], cache_v=cache_v[l])
        hs, k, v, u_ext = _layer(hs, pos_s, valid_s, state_conv[l], attend_s, norm_w[l], w_in[l],
                                 q_norm_w[l], k_norm_w[l], sinks[l], conv_w[l],
                                 w_proj_a[l], w_proj_b[l], w_out[l])
        nk_s.append(jnp.concatenate([cache_k[l].astype(k.dtype), k], axis=1)[:, -WINDOW:])
        nv_s.append(jnp.concatenate([cache_v[l].astype(v.dtype), v], axis=1)[:, -WINDOW:])
        nc_s.append(u_ext[:, -(CONV_K - 1):])
    y_sample = hs

    new_k_prompt = jnp.stack(nk_p, 0)
    new_v_prompt = jnp.stack(nv_p, 0)
    new_conv_prompt = jnp.stack(nc_p, 0)
    new_k_sample = jnp.stack(nk_s, 0)
    new_v_sample = jnp.stack(nv_s, 0)
    new_conv_sample = jnp.stack(nc_s, 0)
    return (y_prompt, y_sample, new_k_prompt, new_v_prompt, new_conv_prompt,
            new_k_sample, new_v_sample, new_conv_sample)
```

```python
import contextlib
import numpy as np
import concourse.bass as bass
import concourse.mybir as mybir
from concourse.bass_utils import run_bass_kernel_spmd

F32 = mybir.dt.float32
BF16 = mybir.dt.bfloat16
AF = mybir.ActivationFunctionType
ALU = mybir.AluOpType
AX = mybir.AxisListType

D = 1024
NCORES = 8
NU = 18
G = 6
NSEQ = 16
EPS = 1e-6
PAST_LEN = 16384
N_META = 16
IN_W = 8704


class Buf:
    __slots__ = ("name", "t", "last_w", "readers", "ld_stream", "st_stream", "excl")

    def __init__(self, name, t=None, excl=False):
        self.name = name
        self.t = t
        self.excl = excl
        self.last_w = None
        self.readers = {}
        self.ld_stream = None
        self.st_stream = None

    def __getitem__(self, k):
        return self.t[k]


class Sched:
    def __init__(self, nc):
        self.nc = nc
        self.ops = []
        self.eng_pos = {}
        self.nstream = 0
        self.dma_streams = set()
        self.engs = {"pe": nc.tensor, "act": nc.scalar, "dve": nc.vector,
                     "pool": nc.gpsimd, "sp": nc.sync}

    def new_stream(self, name):
        s = "d%d_%s" % (self.nstream, name)
        self.nstream += 1
        self.dma_streams.add(s)
        return s

    def _rec(self, eng, stream, fn, reads, writes, kind):
        ex = [b for b in reads if b.excl and b not in writes]
        if ex:
            reads = [b for b in reads if not b.excl]
            writes = list(writes) + ex
        oid = len(self.ops)
        deps = set()
        for b in reads:
            if b.last_w is not None:
                deps.add(b.last_w)
        for b in writes:
            if b.last_w is not None:
                deps.add(b.last_w)
            for r in b.readers.values():
                deps.add(r)
        pos = self.eng_pos.get(eng, 0)
        self.eng_pos[eng] = pos + 1
        self.ops.append({"eng": eng, "stream": stream, "fn": fn, "deps": deps,
                         "kind": kind, "epos": pos})
        for b in reads:
            if b not in writes:
                b.readers[stream] = oid
        for b in writes:
            b.last_w = oid
            b.readers = {}
        return oid

    def op(self, eng, fn, reads=(), writes=()):
        return self._rec(eng, eng, fn, list(reads), list(writes), "eng")

    def dma(self, fn, reads=(), writes=(), queue="sp", stream=None):
        reads = list(reads)
        writes = list(writes)
        if stream is None:
            if writes:
                b = writes[0]
                if b.ld_stream is None:
                    b.ld_stream = self.new_stream("ld_" + b.name)
                stream = b.ld_stream
            else:
                b = reads[0]
                if b.st_stream is None:
                    b.st_stream = self.new_stream("st_" + b.name)
                stream = b.st_stream
        return self._rec(queue, stream, fn, reads, writes, "dma")

    def fence(self, bufs, engines=("pe", "act", "dve", "pool", "sp")):
        for e in engines:
            self._rec(e, e, None, list(bufs), [], "nop")
        for b in bufs:
            b.last_w = None
            b.readers = {}

    def barrier(self, bufs, engines=("pe", "act", "dve", "pool", "sp")):
        for e in engines:
            oid = len(self.ops)
            deps = set()
            for b in bufs:
                if b.last_w is not None:
                    deps.add(b.last_w)
                deps.update(b.readers.values())
            pos = self.eng_pos.get(e, 0)
            self.eng_pos[e] = pos + 1
            self.ops.append({"eng": e, "stream": e, "fn": None, "deps": deps, "kind": "nop", "epos": pos})
        for b in bufs:
            b.last_w = None
            b.readers = {}

    def emit(self):
        nc = self.nc
        ops = self.ops
        n = len(ops)
        spos = [0] * n
        cnt = {}
        for i, o in enumerate(ops):
            c = cnt.get(o["stream"], 0) + 1
            cnt[o["stream"]] = c
            spos[i] = c
        waited = {}
        waits = [None] * n
        signal = [False] * n
        for i, o in enumerate(ops):
            e = o["eng"]
            need = {}
            for d in o["deps"]:
                p = ops[d]
                if p["kind"] == "nop":
                    continue
                ps = p["stream"]
                if p["kind"] != "dma" and p["eng"] == e:
                    if e == "pe" or e == "sp":
                        continue
                    if o["epos"] - p["epos"] > 2:
                        continue
                if waited.get((e, ps), 0) >= spos[d]:
                    continue
                if need.get(ps, (0, None))[0] < spos[d]:
                    need[ps] = (spos[d], d)
            wl = []
            for ps, (sp_, d) in need.items():
                waited[(e, ps)] = sp_
                wl.append(d)
                signal[d] = True
            waits[i] = wl
        for i, o in enumerate(ops):
            if o["kind"] == "dma":
                signal[i] = True
        val = [0] * n
        run = {}
        for i, o in enumerate(ops):
            if signal[i]:
                inc = 16 if o["kind"] == "dma" else 1
                run[o["stream"]] = run.get(o["stream"], 0) + inc
                val[i] = run[o["stream"]]
        streams = sorted(set(o["stream"] for i, o in enumerate(ops) if signal[i]))
        sems = {}
        with contextlib.ExitStack() as st:
            for s in streams:
                sems[s] = st.enter_context(nc.semaphore("s_" + s))
            for i, o in enumerate(ops):
                eng = self.engs[o["eng"]]
                for d in waits[i]:
                    eng.wait_ge(sems[ops[d]["stream"]], val[d])
                if o["fn"] is None:
                    continue
                ins = o["fn"](eng) if o["kind"] == "dma" else o["fn"]()
                if signal[i]:
                    ins.then_inc(sems[o["stream"]], 16 if o["kind"] == "dma" else 1)
            for s in streams:
                if s in self.dma_streams:
                    nc.sync.wait_ge(sems[s], run[s])
        self.stats = {"n_ops": n, "n_waits": sum(len(w) for w in waits),
                      "n_signal": sum(signal), "n_sems": len(streams)}


OFF = {"q": 0, "k": 1024, "v": 1280, "za": 1536, "b": 2560, "c": 3584, "h": 4608,
       "zb": 5632, "ga": 6656, "gb": 7680}

STAGES = [
    ("c0", "win", OFF["c"]), ("h0", "win", OFF["h"]),
    ("c1", "win", OFF["c"] + 512), ("h1", "win", OFF["h"] + 512),
    ("b0", "win", OFF["b"]), ("b1", "win", OFF["b"] + 512),
    ("zb0", "win", OFF["zb"]), ("zb1", "win", OFF["zb"] + 512),
    ("q0", "win", OFF["q"]), ("q1", "win", OFF["q"] + 512),
    ("kv", "win", OFF["k"]),
    ("za0", "win", OFF["za"]), ("za1", "win", OFF["za"] + 512),
    ("ga0", "win", OFF["ga"]), ("pa0", "wpa", 0), ("gb0", "win", OFF["gb"]), ("pb0", "wpb", 0),
    ("ga1", "win", OFF["ga"] + 512), ("pa1", "wpa", 512), ("gb1", "win", OFF["gb"] + 512), ("pb1", "wpb", 512),
    ("wo0", "wout", 0), ("wo1", "wout", 512),
]
HALO_STAGES = ("c0", "h0", "c1", "h1", "kv")


def build_program(limit=None, skip=()):
    nc = bass.Bass("TRN2", target_bir_lowering=False)

    def din(name, shape):
        return nc.dram_tensor(name, shape, F32, kind="ExternalInput").ap()

    def dout(name, shape):
        return nc.dram_tensor(name, shape, F32, kind="ExternalOutput").ap()

    xin = din("xin", [NU * 128, D])
    wsrc = {"win": din("w_in", [D, IN_W]), "wpa": din("w_pa", [D, D]),
            "wpb": din("w_pb", [D, D]), "wout": din("w_out", [D, D])}
    norm_w = din("norm_w", [D])
    qnw = din("q_norm_w", [64])
    knw = din("k_norm_w", [64])
    sinks = din("sinks", [16])
    conv_w = din("conv_w", [3, D])
    cache_k = din("cache_k", [NSEQ, 128, 256])
    cache_v = din("cache_v", [NSEQ, 128, 256])
    state_conv = din("state_conv", [NSEQ * 2, D])
    cosr = din("cosr", [NU * 128, 32])
    sinr = din("sinr", [NU * 128, 32])
    masks_d = din("masks", [20, 128, 128])
    shifts_d = din("shifts", [8, 128, 128])
    ident_d = din("ident", [128, 128])

    NPIECE = len(STAGES)
    wsc = nc.dram_tensor("wsc", [NPIECE, 128, 8 * 512], BF16).ap()

    y_d = dout("y", [17 * 128, D])
    nkp_d = dout("nk_p", [128, 256])
    nvp_d = dout("nv_p", [128, 256])
    ncp_d = dout("nc_p", [2, D])
    nks_d = dout("nk_s", [NSEQ, 128, 256])
    nvs_d = dout("nv_s", [NSEQ, 128, 256])
    ncs_d = dout("nc_s", [NSEQ, 2, D])

    S = Sched(nc)
    with contextlib.ExitStack() as st:
        def sb(name, shape, dt=F32):
            return Buf(name, st.enter_context(nc.sbuf_tensor("s_" + name, shape, dt)))

        def ps(name, shape, dt=F32):
            return Buf(name, st.enter_context(nc.psum_tensor("p_" + name, shape, dt)), excl=True)

        V, A, P, T = nc.vector, nc.scalar, nc.gpsimd, nc.tensor

        acc = [ps("acc%d" % i, [128, 512]) for i in range(2)]
        tp = ps("tp", [128, 8, 128], BF16)
        stb = [ps("st%d" % i, [128, 4, 128]) for i in range(2)]
        ob = [ps("ob%d" % i, [128, 512]) for i in range(3)]
        HB = [(0, 6), (6, 6), (12, 4)]
        tpv = [(tp, tp.t[:])] + [(b_, b_.t[:].bitcast(BF16).rearrange("p a (b c) -> p (a b) c", c=128)) for b_ in stb]
        stq = [(b_, b_.t[:]) for b_ in stb] + [(tp, tp.t[:].bitcast(F32).rearrange("p (a b) c -> p a (b c)", b=2))]
        st_free = [False]
        cpy = [0]

        def tp_pick():
            if st_free[0]:
                return rot(tpv, "tpv")
            return tpv[0]

        def psum_copy(out_ap, in_ap, reads, writes):
            cpy[0] += 1
            if True:
                S.op("act", lambda: A.copy(out=out_ap, in_=in_ap), reads=reads, writes=writes)
            else:
                S.op("dve", lambda: V.tensor_copy(out=out_ap, in_=in_ap), reads=reads, writes=writes)


        def o_head(h):
            bk = 0 if h < 6 else (1 if h < 12 else 2)
            s_ = h - HB[bk][0]
            return ob[bk], ob[bk][:, s_ * 65:(s_ + 1) * 65]

        ident = sb("ident", [128, 128], BF16)
        cs = sb("cs", [128, NU, 32])
        sn = sb("sn", [128, NU, 32])
        masks = sb("masks", [128, 20, 128], BF16)
        shifts = sb("shifts", [128, 8, 128], BF16)
        wbc = [sb("wbc%d" % i, [128, D]) for i in range(3)]
        wq_bc = sb("wq_bc", [128, 64])
        wk_bc = sb("wk_bc", [128, 64])
        esink = sb("esink", [128, 16])
        nw = sb("nw", [128, 8])
        stt = sb("stt", [128, D], BF16)
        ucarry = sb("ucarry", [128, D], BF16)

        stg = [sb("stg%d" % i, [128, 4, 512]) for i in range(3)]
        wpc = [sb("wpc%d" % i, [128, 8, 512], BF16) for i in range(2)]

        xnT = [sb("xnT%d" % i, [128, 8, 128], BF16) for i in range(G)]
        F32a = [sb("Fa%d" % i, [128, 512]) for i in range(G)]
        F32b = [sb("Fb%d" % i, [128, 512]) for i in range(G)]
        BFa = [sb("Ba%d" % i, [128, 512], BF16) for i in range(G)]
        BFb = [sb("Bb%d" % i, [128, 512], BF16) for i in range(G)]
        gbT = [sb("gbT%d" % i, [128, 8, 128], BF16) for i in range(G)]
        gaT = [sb("gaT%d" % i, [128, 8, 128], BF16) for i in range(G)]
        KT2 = [sb("KT%d" % i, [128, 4, 2, 128], BF16) for i in range(G)]
        V65 = [sb("V65_%d" % i, [128, 4, 65], BF16) for i in range(G)]
        FH = [F32a, F32b]
        BH = [BFa, BFb]

        xbuf = [sb("xbuf%d" % i, [128, D]) for i in range(2)]
        ss1 = [sb("ss1_%d" % i, [128, 1]) for i in range(2)]
        rs1 = [sb("rs1_%d" % i, [128, 1]) for i in range(2)]
        xnb = [sb("xnb%d" % i, [128, D], BF16) for i in range(2)]
        tA = [sb("tA%d" % i, [128, 512]) for i in range(2)]
        ssq = [sb("ssq%d" % i, [128, 8]) for i in range(2)]
        rq = [sb("rq%d" % i, [128, 8]) for i in range(2)]
        qn = [sb("qn%d" % i, [128, 512]) for i in range(2)]
        tB, tC = [qn[0]], [qn[1]]
        bevb = [sb("bev%d" % i, [128, 512]) for i in range(1)]
        xh = [sb("xh%d" % i, [128, 512]) for i in range(3)]
        qtab = [sb("qtab%d" % i, [128, NU, 32]) for i in range(4)]
        r1 = [sb("r1_%d" % i, [128, 256]) for i in range(1)]
        r2 = [sb("r2_%d" % i, [128, 256]) for i in range(1)]
        kf = [sb("kf%d" % i, [128, 256]) for i in range(1)]
        vf = [sb("vf%d" % i, [128, 256]) for i in range(1)]
        kdup = [sb("kdup%d" % i, [128, 4, 2, 64], BF16) for i in range(G)]
        PT = [sb("PT%d" % i, [128, 4, 128], BF16) for i in range(4)]
        dsum = [sb("dsum%d" % i, [128, 16]) for i in range(2)]
        rden = [sb("rden%d" % i, [128, 16]) for i in range(2)]
        ckb = [Buf("ckb%d" % i, xbuf[0].t[:, i * 256:(i + 1) * 256]) for i in range(4)]
        cvb = [Buf("cvb%d" % i, xh[i // 2].t[:, (i % 2) * 256:(i % 2 + 1) * 256]) for i in range(4)]
        KTc = [Buf("KTc%d" % i, xbuf[1].t[:, i * 512:(i + 1) * 512].bitcast(BF16).rearrange("p (g v k) -> p g v k", g=4, v=2))
               for i in range(2)]
        V65c = [Buf("V65c%d" % i, xnb[0].t[:, i * 512:i * 512 + 260].rearrange("p (g e) -> p g e", e=65)) for i in range(2)]

        ctr = {}
        deferred = []
        mmc = [0]

        def defer(fn, lag=1):
            deferred.append((mmc[0] + lag, fn))

        def flush(force=False):
            progress = True
            while progress:
                progress = False
                for it in list(deferred):
                    if force or it[0] <= mmc[0]:
                        deferred.remove(it)
                        it[1]()
                        progress = True
                        break

        def rot(lst, key):
            i = ctr.get(key, 0)
            ctr[key] = i + 1
            return lst[i % len(lst)]

        cst = S.new_stream("const")

        def cdma(dst, src_ap, dst_ap=None):
            S.dma(lambda e: e.dma_start(out=(dst[:] if dst_ap is None else dst_ap), in_=src_ap),
                  writes=[dst], stream=cst)

        cdma(xbuf[0], ident_d, xbuf[0][:, 0:128])
        cdma(cs, cosr.rearrange("(u p) f -> p u f", p=128))
        cdma(sn, sinr.rearrange("(u p) f -> p u f", p=128))
        cdma(xbuf[1], shifts_d.rearrange("m p f -> p m f"), xbuf[1][:, 0:1024].rearrange("p (m f) -> p m f", f=128))
        for i in range(3):
            cdma(wbc[i], conv_w[i:i + 1, :].broadcast_to([128, D]))
        cdma(wq_bc, qnw.unsqueeze(0).broadcast_to([128, 64]))
        cdma(wk_bc, knw.unsqueeze(0).broadcast_to([128, 64]))
        cdma(esink, sinks.unsqueeze(0).broadcast_to([128, 16]))
        S.dma(lambda e: e.dma_start(out=nw[:], in_=norm_w.rearrange("(k p) -> p k", p=128),
                                    allow_slow_non_contiguous=True), writes=[nw], stream=cst)
        MSPL = [(0, 7), (7, 7), (14, 6)]
        for i, (m0_, mn_) in enumerate(MSPL):
            cdma(stg[i], masks_d[m0_:m0_ + mn_].rearrange("m p f -> p m f"),
                 stg[i][:, :, :].rearrange("p a b -> p (a b)")[:, 0:mn_ * 128].rearrange("p (m f) -> p m f", f=128))
        cdma(stg[2], state_conv, stg[2][:, :, :].rearrange("p a b -> p (a b)")[0:32, 1024:2048])
        S.fence([xbuf[0], xbuf[1], cs, sn, wbc[0], wbc[1], wbc[2], wq_bc, wk_bc, esink, nw] + stg)
        S.op("dve", lambda: V.tensor_copy(out=ident[:], in_=xbuf[0][:, 0:128]), reads=[xbuf[0]], writes=[ident])
        S.op("dve", lambda: V.tensor_copy(out=shifts[:].rearrange("p m f -> p (m f)"), in_=xbuf[1][:, 0:1024]),
             reads=[xbuf[1]], writes=[shifts])
        for i, (m0_, mn_) in enumerate(MSPL):
            S.op("pool", lambda i=i, m0_=m0_, mn_=mn_: P.tensor_copy(
                out=masks[:, m0_:m0_ + mn_, :].rearrange("p m f -> p (m f)"),
                in_=stg[i][:, :, :].rearrange("p a b -> p (a b)")[:, 0:mn_ * 128]), reads=[stg[i]], writes=[masks])
        S.op("pool", lambda: P.memset(stt[:], 0.0), writes=[stt])
        S.op("pool", lambda: P.tensor_copy(out=stt[0:32, :], in_=stg[2][:, :, :].rearrange("p a b -> p (a b)")[0:32, 1024:2048]),
             reads=[stg[2]], writes=[stt])
        S.op("pool", lambda: P.memset(ucarry[:], 0.0), writes=[ucarry])
        S.op("act", lambda: A.activation(out=esink[:], in_=esink[:], func=AF.Exp), reads=[esink], writes=[esink])
        for ti, (tb_, lo) in enumerate([(cs, 0), (sn, 32), (cs, 32), (sn, 0)]):
            S.op("dve", lambda ti=ti, tb_=tb_, lo=lo: V.tensor_tensor(
                out=qtab[ti][:], in0=tb_[:], in1=wq_bc[:, lo:lo + 32].unsqueeze(1).broadcast_to([128, NU, 32]),
                op=ALU.mult), reads=[tb_, wq_bc], writes=[qtab[ti]])
        for kt_ in KT2:
            S.op("pool", lambda kt_=kt_: P.memset(kt_[:], 0.0), writes=[kt_])
        for i in range(G):
            S.op("pool", lambda i=i: P.memset(V65[i][:, :, 64:65], 1.0), writes=[V65[i]])

        def piece_dma_half(src, col0, half):
            w = wsrc[src]
            sg = rot(stg, "stg")
            S.dma(lambda e: e.dma_start(
                out=sg[:], in_=w[half * 512:(half + 1) * 512, col0:col0 + 512].rearrange("(k p) n -> p k n", p=128)),
                writes=[sg])
            return sg

        def piece_cast(src, hs):
            wp = rot(wpc, "wpc")
            opsl = []
            for half in range(2):
                sg = hs[half]
                if src == "win":
                    for kk in range(4):
                        k = half * 4 + kk
                        if False:
                            opsl.append(lambda sg=sg, kk=kk, k=k: S.op("pool", lambda: P.tensor_scalar(
                                out=wp[:, k, :], in0=sg[:, kk, :], scalar1=nw[:, k:k + 1], scalar2=1.0,
                                op0=ALU.mult, op1=ALU.mult), reads=[sg, nw], writes=[wp]))
                        else:
                            opsl.append(lambda sg=sg, kk=kk, k=k: S.op("act", lambda: A.activation(
                                out=wp[:, k, :], in_=sg[:, kk, :], func=AF.Copy, scale=nw[:, k:k + 1]),
                                reads=[sg, nw], writes=[wp]))
                else:
                    for q2 in range(2):
                        opsl.append(lambda sg=sg, half=half, q2=q2: S.op("dve", lambda: V.tensor_copy(
                            out=wp[:, half * 4 + q2 * 2:half * 4 + q2 * 2 + 2, :], in_=sg[:, q2 * 2:q2 * 2 + 2, :]),
                            reads=[sg], writes=[wp]))
            return wp, opsl

        def kind(u):
            return "halo" if u == 0 else ("sample" if u == 17 else "prompt")

        def transposes(src_fn, dstT, srcbufs, n=8, dst_sl=None):
            tb, tvw = tp_pick()
            for k in range(n):
                S.op("pe", lambda k=k: T.transpose(out=tvw[:, k, :], in_=src_fn(k), identity=ident[:]),
                     reads=srcbufs + [ident], writes=[tb])
            psum_copy(dstT[:] if dst_sl is None else dst_sl, tvw[:, 0:n, :], [tb], [dstT])

        def stage_x_load(u):
            xb = rot(xbuf, "xbuf")
            S.dma(lambda e: e.dma_start(out=xb[:], in_=xin[u * 128:(u + 1) * 128, :]), writes=[xb])
            return xb

        def stage_x_p1(u, xb):
            ss = rot(ss1, "ss1")
            rs = rot(rs1, "rs1")
            xn_ = rot(xnb, "xnb")
            S.op("act", lambda: A.activation(out=xn_[:], in_=xb[:], func=AF.Square, accum_out=ss[:]),
                 reads=[xb], writes=[xn_, ss])
            S.op("act", lambda: A.activation(out=rs[:], in_=ss[:], func=AF.Sqrt, bias=EPS, scale=1.0 / D),
                 reads=[ss], writes=[rs])
            return rs, xn_

        def stage_x_p2(u, xb, rs, xn_):
            S.op("dve", lambda: V.reciprocal(out=rs[:], in_=rs[:]), reads=[rs], writes=[rs])
            S.op("act", lambda: A.activation(out=xn_[:], in_=xb[:], func=AF.Copy, scale=rs[:, 0:1]),
                 reads=[xb, rs], writes=[xn_])

        def stage_x_front(u, xb):
            rs, xn_ = stage_x_p1(u, xb)
            stage_x_p2(u, xb, rs, xn_)
            return xn_

        def xpre_trigger(un):
            xb = stage_x_load(un)

            def p1():
                rs, xn_ = stage_x_p1(un, xb)

                def p2():
                    stage_x_p2(un, xb, rs, xn_)
                    defer(lambda: stage_x_back(un, xn_), lag=2)
                defer(p2, lag=1)
            defer(p1, lag=2)

        def stage_x_back(u, xn_):
            transposes(lambda k: xn_[:, k * 128:(k + 1) * 128], xnT[u % G], [xn_])

        def stage_x(u):
            stage_x_back(u, stage_x_front(u, stage_x_load(u)))

        acc3 = acc + [ob[2]]
        acc_wide = [False]

        def main_mm(u, lhsT, wpair):
            wpb, wpv = wpair
            a = rot(acc3, "acc3") if acc_wide[0] else rot(acc, "acc")
            for k in range(8):
                S.op("pe", lambda k=k: T.matmul(a[:], lhsT=lhsT[:, k, :], rhs=wpv[:, k, :], start=(k == 0), stop=(k == 7)),
                     reads=[lhsT, wpb], writes=[a])
            return a

        def ep_c(u, hf, a):
            f = FH[hf][u % G]
            S.op("act", lambda: A.copy(out=f[:], in_=a[:]), reads=[a], writes=[f])

        def ep_h(u, hf, a):
            s = u % G
            f = FH[hf][s]
            bh = BH[hf][s]
            S.op("dve", lambda: V.tensor_tensor(out=f[:], in0=f[:], in1=a[:], op=ALU.mult), reads=[f, a], writes=[f])
            S.op("act", lambda: A.copy(out=bh[:], in_=f[:]), reads=[f], writes=[bh])
            if u == 16:
                S.dma(lambda e: e.dma_start(out=ncp_d[:, hf * 512:(hf + 1) * 512], in_=f[126:128, :]), reads=[f])
            if u == 17:
                for sq_ in range(NSEQ):
                    S.dma(lambda e, sq_=sq_: e.dma_start(out=ncs_d[sq_, :, hf * 512:(hf + 1) * 512],
                                                         in_=f[sq_ * 8 + 6:sq_ * 8 + 8, :]), reads=[f])

        shsel = {}

        def flat(b_):
            return b_[:].rearrange("p a b -> p (a b)") if len(b_[:].shape) == 3 else b_[:]

        def pre_b(u, hf):
            s = u % G
            ub = BH[hf][s]
            smp = (u == 17)
            m0 = 4 if smp else 0
            if smp:
                prev, prev_ap = stt, stt[:, hf * 512:(hf + 1) * 512]
            elif s == 0:
                prev, prev_ap = ucarry, ucarry[:, hf * 512:(hf + 1) * 512]
            else:
                prev, prev_ap = BH[hf][s - 1], BH[hf][s - 1][:]
            par = ctr.get("shpar", 0)
            ctr["shpar"] = par + 1
            sx, sy = (stb[0], stb[1]) if par % 2 == 0 else (ob[0], ob[1])
            shsel[(u, hf)] = (sx, sy)
            S.op("pe", lambda: T.matmul(flat(sx), lhsT=shifts[:, m0 + 0, :], rhs=ub[:],
                                        start=True, stop=False), reads=[shifts, ub], writes=[sx])
            S.op("pe", lambda: T.matmul(flat(sx), lhsT=shifts[:, m0 + 2, :], rhs=prev_ap,
                                        start=False, stop=True), reads=[shifts, prev], writes=[sx])
            S.op("pe", lambda: T.matmul(flat(sy), lhsT=shifts[:, m0 + 1, :], rhs=ub[:],
                                        start=True, stop=False), reads=[shifts, ub], writes=[sy])
            S.op("pe", lambda: T.matmul(flat(sy), lhsT=shifts[:, m0 + 3, :], rhs=prev_ap,
                                        start=False, stop=True), reads=[shifts, prev], writes=[sy])

        def ep_b(u, hf, a):
            s = u % G
            f = FH[hf][s]
            sx, sy = shsel.pop((u, hf))
            SX, SY = [sx], [sy]
            bev = rot(bevb, "bev")
            S.op("act", lambda: A.copy(out=bev[:], in_=a[:]), reads=[a], writes=[bev])
            t1 = rot(tA, "tA")
            t2 = rot(tB, "tB")
            t3 = rot(tC, "tC")
            cs_ = slice(hf * 512, (hf + 1) * 512)
            S.op("dve", lambda: V.tensor_tensor(out=t1[:], in0=flat(sx), in1=wbc[1][:, cs_],
                                                op=ALU.mult), reads=SX + [wbc[1]], writes=[t1])
            S.op("dve", lambda: V.tensor_tensor(out=t2[:], in0=flat(sy), in1=wbc[0][:, cs_],
                                                op=ALU.mult), reads=SY + [wbc[0]], writes=[t2])
            S.op("pool", lambda: P.tensor_tensor(out=t3[:], in0=f[:], in1=wbc[2][:, cs_], op=ALU.mult),
                 reads=[f, wbc[2]], writes=[t3])
            S.op("dve", lambda: V.tensor_tensor(out=t1[:], in0=t1[:], in1=t2[:], op=ALU.add), reads=[t1, t2], writes=[t1])
            S.op("pool", lambda: P.tensor_tensor(out=t1[:], in0=t1[:], in1=t3[:], op=ALU.add), reads=[t1, t3], writes=[t1])
            S.op("dve", lambda: V.tensor_tensor(out=f[:], in0=t1[:], in1=bev[:], op=ALU.mult), reads=[t1, bev], writes=[f])

        def ep_zb(u, hf, a):
            s = u % G
            f = FH[hf][s]
            bh = BH[hf][s]
            t1 = rot(tA, "tA")
            S.op("act", lambda: A.activation(out=t1[:], in_=a[:], func=AF.Silu), reads=[a], writes=[t1])
            S.op("pool", lambda: P.tensor_tensor(out=bh[:], in0=f[:], in1=t1[:], op=ALU.mult), reads=[f, t1], writes=[bh])

        def half_transposes(u, dstT):
            s = u % G
            tb, tvw = tp_pick()
            for hf in range(2):
                bh = BH[hf][s]
                for kk in range(4):
                    k = hf * 4 + kk
                    S.op("pe", lambda k=k, kk=kk, bh=bh: T.transpose(out=tvw[:, k, :], in_=bh[:, kk * 128:(kk + 1) * 128],
                                                                     identity=ident[:]), reads=[bh, ident], writes=[tb])
            psum_copy(dstT[:], tvw[:, :, :], [tb], [dstT])

        def norm_rope(u, a_ap, a_buf, nh, w_bc, out_ap_fn, out_bufs):
            sq_ = rot(qn, "qn")
            ss = rot(ssq, "ssq")
            r = rot(rq, "rq")
            q_ = rot(qn, "qn")
            w = nh * 64
            S.op("act", lambda: A.activation(out=sq_[:, 0:w], in_=a_ap, func=AF.Square), reads=[a_buf], writes=[sq_])
            S.op("dve", lambda: V.tensor_reduce(out=ss[:, 0:nh], in_=sq_[:, 0:w].rearrange("p (h d) -> p h d", d=64),
                                                axis=AX.X, op=ALU.add), reads=[sq_], writes=[ss])
            S.op("act", lambda: A.activation(out=r[:, 0:nh], in_=ss[:, 0:nh], func=AF.Sqrt, bias=EPS, scale=1.0 / 64),
                 reads=[ss], writes=[r])
            S.op("dve", lambda: V.reciprocal(out=r[:, 0:nh], in_=r[:, 0:nh]), reads=[r], writes=[r])
            S.op("dve", lambda: V.tensor_tensor(
                out=q_[:, 0:w].rearrange("p (h d) -> p h d", d=64), in0=a_ap.rearrange("p (h d) -> p h d", d=64),
                in1=r[:, 0:nh].unsqueeze(2).broadcast_to([128, nh, 64]), op=ALU.mult), reads=[a_buf, r], writes=[q_])
            if w_bc is not None:
                S.op("pool", lambda: P.tensor_tensor(
                    out=q_[:, 0:w].rearrange("p (h d) -> p h d", d=64), in0=q_[:, 0:w].rearrange("p (h d) -> p h d", d=64),
                    in1=w_bc[:, :].unsqueeze(1).broadcast_to([128, nh, 64]), op=ALU.mult), reads=[q_, w_bc], writes=[q_])
            qv = q_[:, 0:w].rearrange("p (h t d) -> p h t d", t=2, d=32)
            x1 = qv[:, :, 0, :]
            x2 = qv[:, :, 1, :]
            if w_bc is not None:
                tbs = [cs, sn, cs, sn]
            else:
                tbs = qtab
            tv = [t_[:, u, :].unsqueeze(1).broadcast_to([128, nh, 32]) for t_ in tbs]
            cb, sb_, cb2, sb2 = tv[0], tv[1], tv[2], tv[3]
            cs_b, sn_b, cs_b2, sn_b2 = tbs
            a1, a2 = r1[0], r2[0]
            a3, a4 = a1, a2

            def v3(t):
                return t[:, 0:nh * 32].rearrange("p (h d) -> p h d", d=32)
            S.op("dve", lambda: V.tensor_tensor(out=v3(a1), in0=x1, in1=cb, op=ALU.mult), reads=[q_, cs_b], writes=[a1])
            S.op("pool", lambda: P.tensor_tensor(out=v3(a2), in0=x2, in1=sb_, op=ALU.mult), reads=[q_, sn_b], writes=[a2])
            S.op("dve", lambda: V.tensor_tensor(out=out_ap_fn(0), in0=v3(a1), in1=v3(a2), op=ALU.subtract),
                 reads=[a1, a2], writes=out_bufs)
            S.op("dve", lambda: V.tensor_tensor(out=v3(a3), in0=x2, in1=cb2, op=ALU.mult), reads=[q_, cs_b2], writes=[a3])
            S.op("pool", lambda: P.tensor_tensor(out=v3(a4), in0=x1, in1=sb2, op=ALU.mult), reads=[q_, sn_b2], writes=[a4])
            S.op("pool", lambda: P.tensor_tensor(out=out_ap_fn(1), in0=v3(a3), in1=v3(a4), op=ALU.add),
                 reads=[a3, a4], writes=out_bufs)

        def ep_q(u, hf, a):
            s = u % G
            bh = BH[hf][s]
            bv = bh[:].rearrange("p (h t d) -> p h t d", t=2, d=32)
            norm_rope(u, a[:], a, 8, None, lambda part: bv[:, :, part, :], [bh])

        def q_transposes(u):
            s = u % G
            half_transposes(u, gaT[s])

        def kt_transposes(kd, ktdst):
            for g in range(4):
                S.op("pe", lambda g=g: T.transpose(out=tp[:, g, :], in_=kd[:, g, :, :].rearrange("p a d -> p (a d)"),
                                                   identity=ident[:]), reads=[kd, ident], writes=[tp])
            S.op("dve", lambda: V.tensor_copy(out=ktdst[0:64, :, 0, :], in_=tp[0:64, 0:4, :]), reads=[tp], writes=[ktdst])
            S.op("dve", lambda: V.tensor_copy(out=ktdst[64:128, :, 1, :], in_=tp[64:128, 0:4, :]), reads=[tp], writes=[ktdst])

        def cast_kv(kf_, vf_, kd, v65dst):
            S.op("act", lambda: A.copy(out=kd[:], in_=kf_[:].rearrange("p (g d) -> p g d", d=64).unsqueeze(2).broadcast_to([128, 4, 2, 64])),
                 reads=[kf_], writes=[kd])
            S.op("dve", lambda: V.tensor_copy(out=v65dst[:, :, 0:64], in_=vf_[:].rearrange("p (g d) -> p g d", d=64)),
                 reads=[vf_], writes=[v65dst])

        def ep_kv(u, a):
            s = u % G
            kd = kdup[s]
            if u in (16, 17):
                kf_, vf_ = kf[0], vf[0]
                kv3 = kf_[:].rearrange("p (h t d) -> p h t d", t=2, d=32)
                norm_rope(u, a[:, 0:256], a, 4, wk_bc, lambda part: kv3[:, :, part, :], [kf_])
                S.op("act", lambda: A.copy(out=vf_[:], in_=a[:, 256:512]), reads=[a], writes=[vf_])
                cast_kv(kf_, vf_, kd, V65[s])
                if u == 16:
                    S.dma(lambda e: e.dma_start(out=nkp_d, in_=kf_[:]), reads=[kf_])
                    S.dma(lambda e: e.dma_start(out=nvp_d, in_=vf_[:]), reads=[vf_])
                else:
                    for sq_ in range(NSEQ):
                        S.dma(lambda e, sq_=sq_: e.dma_start(out=nks_d[sq_, 120:128, :], in_=kf_[sq_ * 8:(sq_ + 1) * 8, :]),
                              reads=[kf_])
                        S.dma(lambda e, sq_=sq_: e.dma_start(out=nvs_d[sq_, 120:128, :], in_=vf_[sq_ * 8:(sq_ + 1) * 8, :]),
                              reads=[vf_])
            else:
                kd0 = kd[:, :, 0, :].rearrange("p g (t d) -> p g t d", t=2)
                S.op("dve", lambda: V.tensor_copy(out=V65[s][:, :, 0:64], in_=a[:, 256:512].rearrange("p (g d) -> p g d", d=64)),
                     reads=[a], writes=[V65[s]])
                norm_rope(u, a[:, 0:256], a, 4, wk_bc, lambda part: kd0[:, :, part, :], [kd])
                S.op("pool", lambda: P.tensor_copy(out=kd[:, :, 1, :], in_=kd[:, :, 0, :]), reads=[kd], writes=[kd])

        def attention(u, keyblocks, loaders=None, mid=None):
            s = u % G
            QT = gaT[s]
            nkb = len(keyblocks)
            items = [(kb, g) for kb in range(nkb) for g in range(4)]
            pts = {}
            kbd = {}

            def need(kb):
                if kb < nkb and kb not in kbd:
                    kbd[kb] = keyblocks[kb]()

            def scores(i):
                kb, g = items[i]
                if loaders is not None and g == 0:
                    for kb2 in range(kb, min(kb + 4, len(loaders))):
                        loaders[kb2]()
                need(kb)
                if g == 3:
                    need(kb + 1)
                KT, _, mi = kbd[kb]
                stt_, stv = rot(stq, "stq")
                pt = rot(PT, "PT")
                pts[i] = pt
                S.op("pe", lambda: T.matmul(stv, lhsT=ident[:],
                                            rhs=masks[:, mi, :].unsqueeze(1).broadcast_to([128, 4, 128]),
                                            start=True, stop=False), reads=[ident, masks], writes=[stt_])
                for var in range(2):
                    S.op("pe", lambda var=var: T.matmul(
                        stv[:, 2 * var:2 * var + 2, :], lhsT=KT[:, g, var, :],
                        rhs=QT[:, 2 * g:2 * g + 2, :], start=False, stop=(var == 1)),
                        reads=[KT, QT], writes=[stt_])
                if "exp" in skip:
                    return
                S.op("act", lambda: A.activation(out=pt[:].rearrange("p a b -> p (a b)"),
                                                 in_=stv.rearrange("p a b -> p (a b)"), func=AF.Exp, scale=0.125),
                     reads=[stt_], writes=[pt])
                pts[i] = pt

            def bank_of(h):
                return 0 if h < 6 else (1 if h < 12 else 2)

            def head_of(g, j):
                return 4 * g + 2 * (j % 2) + (j // 2)
            first_w, last_w = {}, {}
            for i_, (kb_, g_) in enumerate(items):
                for j_ in range(4):
                    bk_ = bank_of(head_of(g_, j_))
                    first_w.setdefault(bk_, (i_, j_))
                    last_w[bk_] = (i_, j_)

            def pv(i):
                kb, g = items[i]
                _, VV, _ = kbd[kb]
                pt = pts.pop(i)
                for j in range(4):
                    if "pv" in skip:
                        break
                    h = head_of(g, j)
                    bk = bank_of(h)
                    obuf, oap = o_head(h)
                    S.op("pe", lambda j=j, oap=oap, bk=bk: T.matmul(oap, lhsT=pt[:, j, :], rhs=VV[:, g, :],
                                                                    start=(first_w[bk] == (i, j)),
                                                                    stop=(last_w[bk] == (i, j))),
                         reads=[pt, VV], writes=[obuf])

            n = len(items)
            LAG = 3
            for i in range(n + LAG):
                if i < n:
                    scores(i)
                if i - LAG >= 0:
                    pv(i - LAG)
            if mid is not None:
                mid()
            if "onorm" in skip:
                return
            ds = rot(dsum, "dsum")
            rd = rot(rden, "rden")
            for bk, (h0, nh) in enumerate(HB):
                S.op("dve", lambda bk=bk, h0=h0, nh=nh: V.tensor_tensor(
                    out=ds[:, h0:h0 + nh], in0=ob[bk][:, 0:nh * 65].rearrange("p (h e) -> p h e", e=65)[:, :, 64],
                    in1=esink[:, h0:h0 + nh], op=ALU.add), reads=[ob[bk], esink], writes=[ds])
            S.op("dve", lambda: V.reciprocal(out=rd[:], in_=ds[:]), reads=[ds], writes=[rd])
            segs = [(0, 0, 6, F32a[s], 0), (1, 0, 2, F32a[s], 384), (1, 2, 4, F32b[s], 0), (2, 0, 4, F32b[s], 256)]
            for (bk, sl0, nh, dst, c0) in segs:
                h0 = HB[bk][0] + sl0
                S.op("dve", lambda bk=bk, sl0=sl0, nh=nh, dst=dst, c0=c0, h0=h0: V.tensor_tensor(
                    out=dst[:, c0:c0 + nh * 64].rearrange("p (h d) -> p h d", d=64),
                    in0=ob[bk][:, sl0 * 65:(sl0 + nh) * 65].rearrange("p (h e) -> p h e", e=65)[:, :, 0:64],
                    in1=rd[:, h0:h0 + nh].unsqueeze(2).broadcast_to([128, nh, 64]), op=ALU.mult),
                    reads=[ob[bk], rd], writes=[dst])

        cache_ld = {}
        cache_issued = set()

        def cache_loader(sq_):
            def ld():
                if sq_ in cache_issued:
                    return
                cache_issued.add(sq_)
                ck = rot(ckb, "ckb")
                cv = rot(cvb, "cvb")
                S.dma(lambda e: e.dma_start(out=ck[:], in_=cache_k[sq_]), writes=[ck])
                S.dma(lambda e: e.dma_start(out=cv[:], in_=cache_v[sq_]), writes=[cv])
                cache_ld[sq_] = (ck, cv)
            return ld

        def cache_provider(sq_):
            def prov():
                cache_loader(sq_)()
                ck, cv = cache_ld.pop(sq_)
                kt = rot(KTc, "KTc")
                vv = rot(V65c, "V65c")
                kd = rot(kdup[0:2], "kdc")
                cast_kv(ck, cv, kd, vv)
                kt_transposes(kd, kt)
                return (kt, vv, 4 + sq_)
            return prov

        def ep_za(u, hf, a):
            s = u % G
            f = FH[hf][s]
            bh = BH[hf][s]
            t1 = rot(tA, "tA")
            S.op("act", lambda: A.activation(out=t1[:], in_=a[:], func=AF.Silu), reads=[a], writes=[t1])
            S.op("pool", lambda: P.tensor_tensor(out=bh[:], in0=f[:], in1=t1[:], op=ALU.mult), reads=[f, t1], writes=[bh])

        def ep_ga(u, hf, a):
            f = F32a[u % G]
            S.op("act", lambda: A.activation(out=f[:], in_=a[:], func=AF.Sigmoid), reads=[a], writes=[f])

        def ep_pa(u, hf, a):
            f = F32a[u % G]
            S.op("dve", lambda: V.tensor_tensor(out=f[:], in0=f[:], in1=a[:], op=ALU.mult), reads=[f, a], writes=[f])

        def ep_gb(u, hf, a):
            f = F32b[u % G]
            S.op("act", lambda: A.activation(out=f[:], in_=a[:], func=AF.Sigmoid), reads=[a], writes=[f])

        def ep_pb(u, hf, a):
            s = u % G
            t1 = rot(tA, "tA")
            bh = BH[hf][s]
            S.op("dve", lambda: V.tensor_tensor(out=t1[:], in0=F32b[s][:], in1=a[:], op=ALU.mult),
                 reads=[F32b[s], a], writes=[t1])
            S.op("pool", lambda: P.tensor_tensor(out=bh[:], in0=t1[:], in1=F32a[s][:], op=ALU.add),
                 reads=[t1, F32a[s]], writes=[bh])

        wo_buf = {}
        wo_queue = []
        wo_lagq = []

        def wo_load(u, hf):
            xb = rot(xh, "xh")
            wo_buf[(u, hf)] = xb
            S.dma(lambda e: e.dma_start(out=xb[:], in_=xin[u * 128:(u + 1) * 128, hf * 512:(hf + 1) * 512]),
                  writes=[xb])

        def ep_wo(u, hf, a):
            xb = wo_buf.pop((u, hf))
            S.op("dve", lambda: V.tensor_tensor(out=xb[:], in0=a[:], in1=xb[:], op=ALU.add), reads=[a, xb], writes=[xb])
            S.dma(lambda e: e.dma_start(out=y_d[(u - 1) * 128:u * 128, hf * 512:(hf + 1) * 512], in_=xb[:]),
                  reads=[xb])

        dd = S.new_stream("d2d")
        if "d2d" not in skip:
            S.dma(lambda e: e.dma_start(out=nks_d[:, 0:120, :], in_=cache_k[:, 8:128, :]), stream=dd)
            S.dma(lambda e: e.dma_start(out=nvs_d[:, 0:120, :], in_=cache_v[:, 8:128, :]), stream=dd)

        xpre_buf = {}
        XPRE = {"pb1": 0, "wo0": 2, "wo1": 4}
        allst = [(p, si) for p in range(3) for si in range(len(STAGES))]
        pend = {}

        def issue_dma_half(idx, half):
            if idx < len(allst):
                _, si = allst[idx]
                pend.setdefault(idx, []).append(piece_dma_half(STAGES[si][1], STAGES[si][2], half))

        def start_cast(idx):
            if idx >= len(allst):
                return None, []
            _, si = allst[idx]
            return piece_cast(STAGES[si][1], pend.pop(idx))

        NFP = NPIECE + 2
        wsc_all = Buf("wsc_all")
        wsc_stream = S.new_stream("wsc")

        def scratch_write(n, wpb, src):
            if n >= NPIECE:
                return
            q_ = "act"
            S.dma(lambda e: e.dma_start(out=wsc[n], in_=wpb[:].rearrange("p k n -> p (k n)")),
                  reads=[wpb], writes=[wsc_all], queue=q_, stream=wsc_stream)

        def bf_view(b_):
            if b_ in wpc:
                return b_.t[:]
            return b_.t[:].bitcast(BF16).rearrange("p a (b c) -> p (a b) c", c=512)

        ring = [stg[(2 * NFP - 1) % 3], stg[(2 * NFP - 2) % 3 - 0], None, None, None]
        h0_, h1_ = (2 * (NFP - 1)) % 3, (2 * (NFP - 1) + 1) % 3
        free_ = [i for i in range(3) if i not in (h0_, h1_)][0]
        ring = [stg[free_], stg[h0_], stg[h1_], wpc[(NFP - 2) % 2], wpc[(NFP - 1) % 2]]
        slot_free_at = {0: NFP - 2, 1: NFP - 1, 2: NFP - 1, 3: NFP - 1, 4: NFP}
        ring_piece = {}
        next_load = [NFP]

        def ring_loads(idx):
            while next_load[0] < len(allst) and next_load[0] <= idx + 3:
                n = next_load[0]
                k_ = (n - NFP) % 5
                if slot_free_at[k_] > idx:
                    break
                b_ = ring[k_]
                v_ = bf_view(b_)
                S.dma(lambda e, v_=v_, n=n: e.dma_start(out=v_.rearrange("p k n -> p (k n)"), in_=wsc[n % NPIECE]),
                      reads=[wsc_all], writes=[b_])
                ring_piece[n] = (b_, v_)
                slot_free_at[k_] = n + 1
                next_load[0] += 1

        issue_dma_half(0, 0)
        issue_dma_half(0, 1)
        wp0, ops0 = start_cast(0)
        wp_cur = (wp0, wp0.t[:])
        for f_ in ops0:
            f_()
        scratch_write(0, wp0, STAGES[0][1])
        issue_dma_half(1, 0)
        issue_dma_half(1, 1)
        for idx, (p, si) in enumerate(allst):
            if limit is not None and idx >= limit:
                break
            name = STAGES[si][0]
            units = list(range(p * G, (p + 1) * G))
            if si == 0 and p == 0:
                for u in units:
                    stage_x(u)
            wp = wp_cur
            ring_loads(idx)
            if idx + 2 < NFP:
                issue_dma_half(idx + 2, 0)
            if idx + 1 < NFP:
                wpn_, cast_ops = start_cast(idx + 1)
                wp_next = (wpn_, wpn_.t[:]) if wpn_ is not None else None
            else:
                cast_ops = []
                wp_next = ring_piece.pop(idx + 1, None)
            nsrc_ = STAGES[allst[idx + 1][1]][1] if idx + 1 < len(allst) else None
            scratch_done = [idx + 1 >= NFP or not cast_ops]
            n_cast = len(cast_ops)
            n_emitted = 0
            second_half_issued = False
            n_units_here = sum(1 for u_ in units if not (kind(u_) == "halo" and name not in HALO_STAGES))
            base = name.rstrip("01")
            hf = 1 if name.endswith("1") else 0
            st_free[0] = name not in ("b0", "b1", "kv", "za0")
            acc_wide[0] = name not in ("kv", "za0", "za1")
            jproc = 0
            proc_units = [u_ for u_ in units if not (kind(u_) == "halo" and name not in HALO_STAGES)]
            if name == "pb1":
                wo_queue.clear()
                wo_queue.extend([(u_, 0) for u_ in proc_units] + [(u_, 1) for u_ in proc_units])
                for _ in range(len(xh)):
                    wo_load(*wo_queue.pop(0))
            for u in units:
                s = u % G
                kd = kind(u)
                if kd == "halo" and name not in HALO_STAGES:
                    continue
                if base in ("pa",):
                    lhs = gaT[s]
                elif base in ("pb",):
                    lhs = gbT[s]
                elif base == "wo":
                    lhs = gbT[s]
                else:
                    lhs = xnT[s]
                if base == "b":
                    pre_b(u, hf)
                a = main_mm(u, lhs, wp)
                mmc[0] += 1
                flush()
                jproc += 1
                if p < 2 and name in XPRE and jproc in (1, 3):
                    xpre_trigger((p + 1) * G + XPRE[name] + (1 if jproc == 3 else 0))
                if base == "c":
                    ep_c(u, hf, a)
                elif base == "h":
                    ep_h(u, hf, a)
                elif base == "b":
                    ep_b(u, hf, a)
                    if hf == 1 and s == G - 1 and u != 17:
                        S.op("pool", lambda s=s: P.tensor_copy(out=ucarry[:, 0:512], in_=BFa[s][:]),
                             reads=[BFa[s]], writes=[ucarry])
                        S.op("pool", lambda s=s: P.tensor_copy(out=ucarry[:, 512:1024], in_=BFb[s][:]),
                             reads=[BFb[s]], writes=[ucarry])
                elif base == "zb":
                    ep_zb(u, hf, a)
                    if hf == 1:
                        defer(lambda u=u, s=s: half_transposes(u, gbT[s]), lag=2)
                elif base == "q":
                    ep_q(u, hf, a)
                    if hf == 1:
                        defer(lambda u=u: q_transposes(u), lag=3)
                elif name == "kv":
                    ep_kv(u, a)

                    jk = proc_units.index(u)
                    nxt_u = proc_units[jk + 1] if jk + 1 < len(proc_units) else None

                    def kv_aux(u=u, s=s, kd=kd, jk=jk, nxt_u=nxt_u):
                        kt_done = [False]

                        def kt_next():
                            if nxt_u is not None:
                                kt_transposes(kdup[nxt_u % G], KT2[nxt_u % G])
                        if jk == 0:
                            kt_transposes(kdup[s], KT2[s])
                        if "attn" in skip:
                            pass
                        elif kd == "prompt":
                            ps_ = (u - 1) % G
                            mprev = 2 if u == 1 else 0
                            attention(u, [lambda: (KT2[ps_], V65[ps_], mprev),
                                          lambda: (KT2[s], V65[s], 1)], mid=kt_next)
                            kt_done[0] = True
                        elif kd == "sample":
                            S.barrier([xbuf[0], xbuf[1], xnb[0], xh[0], xh[1]])
                            for kt_ in KTc:
                                S.op("pool", lambda kt_=kt_: P.memset(kt_[:], 0.0), writes=[kt_])
                            for vv_ in V65c:
                                S.op("pool", lambda vv_=vv_: P.memset(vv_[:, :, 64:65], 1.0), writes=[vv_])
                            kbs = [cache_provider(sq_) for sq_ in range(NSEQ)]
                            kbs.append(lambda: (KT2[s], V65[s], 3))
                            attention(u, kbs, loaders=[cache_loader(sq_) for sq_ in range(NSEQ)])
                            S.barrier(ckb + cvb + KTc + V65c)
                        if not kt_done[0]:
                            kt_next()
                    defer(kv_aux, lag=4)
                elif base == "za":
                    ep_za(u, hf, a)
                    if hf == 1:
                        defer(lambda u=u, s=s: half_transposes(u, gaT[s]), lag=2)
                elif base == "ga":
                    ep_ga(u, hf, a)
                elif base == "pa":
                    ep_pa(u, hf, a)
                elif base == "gb":
                    ep_gb(u, hf, a)
                elif base == "pb":
                    ep_pb(u, hf, a)
                    if hf == 1:
                        defer(lambda u=u, s=s: half_transposes(u, gbT[s]), lag=2)
                elif base == "wo":
                    ep_wo(u, hf, a)
                    if wo_lagq:
                        wo_load(*wo_lagq.pop(0))
                    if wo_queue:
                        wo_lagq.append(wo_queue.pop(0))
                den_ = max(1, n_units_here - 2)
                tgt = (n_cast * jproc + den_ - 1) // den_
                while n_emitted < min(tgt, n_cast):
                    cast_ops[n_emitted]()
                    n_emitted += 1
                if not second_half_issued and n_emitted >= n_cast // 2:
                    if idx + 2 < NFP:
                        issue_dma_half(idx + 2, 1)
                    second_half_issued = True
                if not scratch_done[0] and n_emitted >= n_cast:
                    scratch_write(idx + 1, wp_next[0], nsrc_)
                    scratch_done[0] = True
            while n_emitted < n_cast:
                cast_ops[n_emitted]()
                n_emitted += 1
            if not scratch_done[0]:
                scratch_write(idx + 1, wp_next[0], nsrc_)
            if not second_half_issued and idx + 2 < NFP:
                issue_dma_half(idx + 2, 1)
            wp_cur = wp_next

        flush(force=True)
        S.emit()
    return nc, S.stats


def _const_tables(core):
    i = np.arange(128)
    j = np.arange(128)[:, None]
    q = np.arange(128)[None, :]
    masks = np.zeros((20, 128, 128), np.float32)
    masks[0] = (j >= q)
    masks[1] = (j <= q)
    mf = (j >= q)
    if core == 0:
        mf = mf & (j >= 128 - N_META)
    masks[2] = mf
    sj, tj = j // 8, j % 8
    sq, tq = q // 8, q % 8
    masks[3] = (sj == sq) & (tj <= tq)
    for s in range(NSEQ):
        masks[4 + s] = (sq == s) & (j >= tq)
    masks = ((masks - 1.0) * 30000.0).astype(np.float32)
    sh = np.zeros((8, 128, 128), np.float32)
    tp_ = np.arange(128)[:, None]
    t = np.arange(128)[None, :]
    sh[0] = (t == tp_ + 1)
    sh[1] = (t == tp_ + 2)
    sh[2] = (tp_ == 127) & (t == 0)
    sh[3] = ((tp_ == 126) & (t == 0)) | ((tp_ == 127) & (t == 1))
    sh[4] = (t == tp_ + 1) & (tp_ // 8 == t // 8)
    sh[5] = (t == tp_ + 2) & (tp_ // 8 == t // 8)
    sh[6] = (tp_ < 32) & (tp_ % 2 == 1) & (t == (tp_ // 2) * 8)
    sh[7] = (tp_ < 32) & (((tp_ % 2 == 0) & (t == (tp_ // 2) * 8)) | ((tp_ % 2 == 1) & (t == (tp_ // 2) * 8 + 1)))
    pos = np.zeros((NU, 128), np.float32)
    base = N_META + core * 2048
    pos[0] = base - 128 + i
    for u in range(1, 17):
        pos[u] = base + (u - 1) * 128 + i
    pos[17] = PAST_LEN + (i % 8)
    inv = np.power(np.float32(10000.0), -np.arange(32, dtype=np.float32) * np.float32(2.0 / 64)).astype(np.float32)
    ang = (pos.reshape(-1, 1).astype(np.float32) * inv[None, :]).astype(np.float32)
    cosr = np.cos(ang.astype(np.float64)).astype(np.float32)
    sinr = np.sin(ang.astype(np.float64)).astype(np.float32)
    return masks, sh, cosr, sinr


_PROGRAM = None


def make_in_maps(x_prompt, x_sample, cache_k, cache_v, state_conv, meta_tokens, norm_w, w_in,
                 q_norm_w, k_norm_w, sinks, conv_w, w_proj_a, w_proj_b, w_out):
    f = lambda a: np.ascontiguousarray(np.asarray(a, dtype=np.float32))
    xp = f(x_prompt)[0]
    xs = f(x_sample)
    ck = f(cache_k)[0].reshape(128, 128, 256)
    cv = f(cache_v)[0].reshape(128, 128, 256)
    sc = f(state_conv)[0]
    meta = f(meta_tokens)
    shared = {"w_in": f(w_in)[0], "w_pa": f(w_proj_a)[0], "w_pb": f(w_proj_b)[0], "w_out": f(w_out)[0],
              "norm_w": f(norm_w)[0], "q_norm_w": f(q_norm_w)[0], "k_norm_w": f(k_norm_w)[0],
              "sinks": f(sinks)[0], "conv_w": f(conv_w)[0], "ident": np.eye(128, dtype=np.float32)}
    in_maps = []
    for c in range(NCORES):
        xin = np.zeros((NU * 128, D), np.float32)
        if c == 0:
            xin[128 - N_META:128] = meta
        else:
            xin[0:128] = xp[c * 2048 - 128:c * 2048]
        xin[128:128 + 2048] = xp[c * 2048:(c + 1) * 2048]
        xin[17 * 128:18 * 128] = xs[c * NSEQ:(c + 1) * NSEQ].reshape(128, D)
        masks, sh, cosr, sinr = _const_tables(c)
        m = dict(shared)
        m.update({"xin": xin, "cache_k": np.ascontiguousarray(ck[c * NSEQ:(c + 1) * NSEQ]),
                  "cache_v": np.ascontiguousarray(cv[c * NSEQ:(c + 1) * NSEQ]),
                  "state_conv": np.ascontiguousarray(sc[c * NSEQ:(c + 1) * NSEQ].reshape(NSEQ * 2, D)),
                  "cosr": cosr, "sinr": sinr, "masks": masks, "shifts": sh})
        in_maps.append(m)
    return in_maps


def kernel(**inputs):
    global _PROGRAM
    in_maps = make_in_maps(**inputs)
    if _PROGRAM is None:
        _PROGRAM = build_program()[0]
    res = run_bass_kernel_spmd(_PROGRAM, in_maps, core_ids=list(range(NCORES)))
    r = res.results
    y_prompt = np.concatenate([r[c]["y"][0:2048] for c in range(NCORES)], 0).reshape(1, 16384, D)
    y_sample = np.concatenate([r[c]["y"][2048:2176].reshape(NSEQ, 8, D) for c in range(NCORES)], 0)
    nk_p = r[NCORES - 1]["nk_p"].reshape(1, 1, 128, 4, 64)
    nv_p = r[NCORES - 1]["nv_p"].reshape(1, 1, 128, 4, 64)
    nc_p = r[NCORES - 1]["nc_p"].reshape(1, 1, 2, D)
    nk_s = np.concatenate([r[c]["nk_s"] for c in range(NCORES)], 0).reshape(1, 128, 128, 4, 64)
    nv_s = np.concatenate([r[c]["nv_s"] for c in range(NCORES)], 0).reshape(1, 128, 128, 4, 64)
    nc_s = np.concatenate([r[c]["nc_s"] for c in range(NCORES)], 0).reshape(1, 128, 2, D)
    return tuple(np.ascontiguousarray(a.astype(np.float32)) for a in
                 (y_prompt, y_sample, nk_p, nv_p, nc_p, nk_s, nv_s, nc_s))
```

```python
import contextlib
import numpy as np
import concourse.bass as bass
import concourse.mybir as mybir
from concourse.bass_utils import run_bass_kernel_spmd

F32 = mybir.dt.float32
BF16 = mybir.dt.bfloat16
AF = mybir.ActivationFunctionType
ALU = mybir.AluOpType
AX = mybir.AxisListType

D = 1024
NCORES = 8
NU = 18
G = 6
NSEQ = 16
EPS = 1e-6
PAST_LEN = 16384
N_META = 16
IN_W = 8704


class Buf:
    __slots__ = ("name", "t", "last_w", "readers", "ld_stream", "st_stream", "excl")

    def __init__(self, name, t=None, excl=False):
        self.name = name
        self.t = t
        self.excl = excl
        self.last_w = None
        self.readers = {}
        self.ld_stream = None
        self.st_stream = None

    def __getitem__(self, k):
        return self.t[k]


class Sched:
    def __init__(self, nc):
        self.nc = nc
        self.ops = []
        self.eng_pos = {}
        self.nstream = 0
        self.dma_streams = set()
        self.engs = {"pe": nc.tensor, "act": nc.scalar, "dve": nc.vector,
                     "pool": nc.gpsimd, "sp": nc.sync}

    def new_stream(self, name):
        s = "d%d_%s" % (self.nstream, name)
        self.nstream += 1
        self.dma_streams.add(s)
        return s

    def _rec(self, eng, stream, fn, reads, writes, kind):
        ex = [b for b in reads if b.excl and b not in writes]
        if ex:
            reads = [b for b in reads if not b.excl]
            writes = list(writes) + ex
        oid = len(self.ops)
        deps = set()
        for b in reads:
            if b.last_w is not None:
                deps.add(b.last_w)
        for b in writes:
            if b.last_w is not None:
                deps.add(b.last_w)
            for r in b.readers.values():
                deps.add(r)
        pos = self.eng_pos.get(eng, 0)
        self.eng_pos[eng] = pos + 1
        self.ops.append({"eng": eng, "stream": stream, "fn": fn, "deps": deps,
                         "kind": kind, "epos": pos})
        for b in reads:
            if b not in writes:
                b.readers[stream] = oid
        for b in writes:
            b.last_w = oid
            b.readers = {}
        return oid

    def op(self, eng, fn, reads=(), writes=()):
        return self._rec(eng, eng, fn, list(reads), list(writes), "eng")

    def dma(self, fn, reads=(), writes=(), queue="sp", stream=None):
        reads = list(reads)
        writes = list(writes)
        if stream is None:
            if writes:
                b = writes[0]
                if b.ld_stream is None:
                    b.ld_stream = self.new_stream("ld_" + b.name)
                stream = b.ld_stream
            else:
                b = reads[0]
                if b.st_stream is None:
                    b.st_stream = self.new_stream("st_" + b.name)
                stream = b.st_stream
        return self._rec(queue, stream, fn, reads, writes, "dma")

    def fence(self, bufs, engines=("pe", "act", "dve", "pool", "sp")):
        for e in engines:
            self._rec(e, e, None, list(bufs), [], "nop")
        for b in bufs:
            b.last_w = None
            b.readers = {}

    def barrier(self, bufs, engines=("pe", "act", "dve", "pool", "sp")):
        for e in engines:
            oid = len(self.ops)
            deps = set()
            for b in bufs:
                if b.last_w is not None:
                    deps.add(b.last_w)
                deps.update(b.readers.values())
            pos = self.eng_pos.get(e, 0)
            self.eng_pos[e] = pos + 1
            self.ops.append({"eng": e, "stream": e, "fn": None, "deps": deps, "kind": "nop", "epos": pos})
        for b in bufs:
            b.last_w = None
            b.readers = {}

    def emit(self):
        nc = self.nc
        ops = self.ops
        n = len(ops)
        spos = [0] * n
        cnt = {}
        for i, o in enumerate(ops):
            c = cnt.get(o["stream"], 0) + 1
            cnt[o["stream"]] = c
            spos[i] = c
        waited = {}
        waits = [None] * n
        signal = [False] * n
        for i, o in enumerate(ops):
            e = o["eng"]
            need = {}
            for d in o["deps"]:
                p = ops[d]
                if p["kind"] == "nop":
                    continue
                ps = p["stream"]
                if p["kind"] != "dma" and p["eng"] == e:
                    if e == "pe" or e == "sp":
                        continue
                    if o["epos"] - p["epos"] > 2:
                        continue
                if waited.get((e, ps), 0) >= spos[d]:
                    continue
                if need.get(ps, (0, None))[0] < spos[d]:
                    need[ps] = (spos[d], d)
            wl = []
            for ps, (sp_, d) in need.items():
                waited[(e, ps)] = sp_
                wl.append(d)
                signal[d] = True
            waits[i] = wl
        for i, o in enumerate(ops):
            if o["kind"] == "dma":
                signal[i] = True
        val = [0] * n
        run = {}
        for i, o in enumerate(ops):
            if signal[i]:
                inc = 16 if o["kind"] == "dma" else 1
                run[o["stream"]] = run.get(o["stream"], 0) + inc
                val[i] = run[o["stream"]]
        streams = sorted(set(o["stream"] for i, o in enumerate(ops) if signal[i]))
        sems = {}
        with contextlib.ExitStack() as st:
            for s in streams:
                sems[s] = st.enter_context(nc.semaphore("s_" + s))
            for i, o in enumerate(ops):
                eng = self.engs[o["eng"]]
                for d in waits[i]:
                    eng.wait_ge(sems[ops[d]["stream"]], val[d])
                if o["fn"] is None:
                    continue
                ins = o["fn"](eng) if o["kind"] == "dma" else o["fn"]()
                if signal[i]:
                    ins.then_inc(sems[o["stream"]], 16 if o["kind"] == "dma" else 1)
            for s in streams:
                if s in self.dma_streams:
                    nc.sync.wait_ge(sems[s], run[s])
        self.stats = {"n_ops": n, "n_waits": sum(len(w) for w in waits),
                      "n_signal": sum(signal), "n_sems": len(streams)}


OFF = {"q": 0, "k": 1024, "v": 1280, "za": 1536, "b": 2560, "c": 3584, "h": 4608,
       "zb": 5632, "ga": 6656, "gb": 7680}

STAGES = [
    ("c0", "win", OFF["c"]), ("h0", "win", OFF["h"]),
    ("c1", "win", OFF["c"] + 512), ("h1", "win", OFF["h"] + 512),
    ("b0", "win", OFF["b"]), ("b1", "win", OFF["b"] + 512),
    ("zb0", "win", OFF["zb"]), ("zb1", "win", OFF["zb"] + 512),
    ("q0", "win", OFF["q"]), ("q1", "win", OFF["q"] + 512),
    ("kv", "win", OFF["k"]),
    ("za0", "win", OFF["za"]), ("za1", "win", OFF["za"] + 512),
    ("ga0", "win", OFF["ga"]), ("pa0", "wpa", 0), ("gb0", "win", OFF["gb"]), ("pb0", "wpb", 0),
    ("ga1", "win", OFF["ga"] + 512), ("pa1", "wpa", 512), ("gb1", "win", OFF["gb"] + 512), ("pb1", "wpb", 512),
    ("wo0", "wout", 0), ("wo1", "wout", 512),
]
HALO_STAGES = ("c0", "h0", "c1", "h1", "kv")


def build_program(limit=None, skip=()):
    nc = bass.Bass("TRN2", target_bir_lowering=False)

    def din(name, shape):
        return nc.dram_tensor(name, shape, F32, kind="ExternalInput").ap()

    def dout(name, shape):
        return nc.dram_tensor(name, shape, F32, kind="ExternalOutput").ap()

    xin = din("xin", [NU * 128, D])
    wsrc = {"win": din("w_in", [D, IN_W]), "wpa": din("w_pa", [D, D]),
            "wpb": din("w_pb", [D, D]), "wout": din("w_out", [D, D])}
    norm_w = din("norm_w", [D])
    qnw = din("q_norm_w", [64])
    knw = din("k_norm_w", [64])
    sinks = din("sinks", [16])
    conv_w = din("conv_w", [3, D])
    cache_k = din("cache_k", [NSEQ, 128, 256])
    cache_v = din("cache_v", [NSEQ, 128, 256])
    state_conv = din("state_conv", [NSEQ * 2, D])
    cosr = din("cosr", [NU * 128, 32])
    sinr = din("sinr", [NU * 128, 32])
    masks_d = din("masks", [20, 128, 128])
    shifts_d = din("shifts", [8, 128, 128])
    ident_d = din("ident", [128, 128])

    NPIECE = len(STAGES)
    wsc = nc.dram_tensor("wsc", [NPIECE, 128, 8 * 512], BF16).ap()

    y_d = dout("y", [17 * 128, D])
    nkp_d = dout("nk_p", [128, 256])
    nvp_d = dout("nv_p", [128, 256])
    ncp_d = dout("nc_p", [2, D])
    nks_d = dout("nk_s", [NSEQ, 128, 256])
    nvs_d = dout("nv_s", [NSEQ, 128, 256])
    ncs_d = dout("nc_s", [NSEQ, 2, D])

    S = Sched(nc)
    with contextlib.ExitStack() as st:
        def sb(name, shape, dt=F32):
            return Buf(name, st.enter_context(nc.sbuf_tensor("s_" + name, shape, dt)))

        def ps(name, shape, dt=F32):
            return Buf(name, st.enter_context(nc.psum_tensor("p_" + name, shape, dt)), excl=True)

        V, A, P, T = nc.vector, nc.scalar, nc.gpsimd, nc.tensor

        acc = [ps("acc%d" % i, [128, 512]) for i in range(2)]
        tp = ps("tp", [128, 8, 128], BF16)
        stb = [ps("st%d" % i, [128, 4, 128]) for i in range(2)]
        ob = [ps("ob%d" % i, [128, 512]) for i in range(3)]
        HB = [(0, 6), (6, 6), (12, 4)]
        tpv = [(tp, tp.t[:])] + [(b_, b_.t[:].bitcast(BF16).rearrange("p a (b c) -> p (a b) c", c=128)) for b_ in stb]
        stq = [(b_, b_.t[:]) for b_ in stb] + [(tp, tp.t[:].bitcast(F32).rearrange("p (a b) c -> p a (b c)", b=2))]
        st_free = [False]
        cpy = [0]

        def tp_pick():
            if st_free[0]:
                return rot(tpv, "tpv")
            return tpv[0]

        def psum_copy(out_ap, in_ap, reads, writes):
            cpy[0] += 1
            if True:
                S.op("act", lambda: A.copy(out=out_ap, in_=in_ap), reads=reads, writes=writes)
            else:
                S.op("dve", lambda: V.tensor_copy(out=out_ap, in_=in_ap), reads=reads, writes=writes)


        def o_head(h):
            bk = 0 if h < 6 else (1 if h < 12 else 2)
            s_ = h - HB[bk][0]
            return ob[bk], ob[bk][:, s_ * 65:(s_ + 1) * 65]

        ident = sb("ident", [128, 128], BF16)
        cs = sb("cs", [128, NU, 32])
        sn = sb("sn", [128, NU, 32])
        masks = sb("masks", [128, 20, 128], BF16)
        shifts = sb("shifts", [128, 8, 128], BF16)
        wbc = [sb("wbc%d" % i, [128, D]) for i in range(3)]
        wq_bc = sb("wq_bc", [128, 64])
        wk_bc = sb("wk_bc", [128, 64])
        esink = sb("esink", [128, 16])
        nw = sb("nw", [128, 8])
        ucarry = sb("ucarry", [128, D], BF16)

        stg = [sb("stg%d" % i, [128, 4, 512]) for i in range(3)]
        wpc = [sb("wpc%d" % i, [128, 8, 512], BF16) for i in range(2)]

        xnT = [sb("xnT%d" % i, [128, 8, 128], BF16) for i in range(G)]
        F32a = [sb("Fa%d" % i, [128, 512]) for i in range(G)]
        F32b = [sb("Fb%d" % i, [128, 512]) for i in range(G)]
        BFa = [sb("Ba%d" % i, [128, 512], BF16) for i in range(G)]
        BFb = [sb("Bb%d" % i, [128, 512], BF16) for i in range(G)]
        gbT = [sb("gbT%d" % i, [128, 8, 128], BF16) for i in range(G)]
        gaT = [sb("gaT%d" % i, [128, 8, 128], BF16) for i in range(G)]
        KT2 = [sb("KT%d" % i, [128, 4, 2, 128], BF16) for i in range(G)]
        V65 = [sb("V65_%d" % i, [128, 4, 65], BF16) for i in range(G)]
        FH = [F32a, F32b]
        BH = [BFa, BFb]

        xbuf = [sb("xbuf%d" % i, [128, D]) for i in range(2)]
        ss1 = [sb("ss1_%d" % i, [128, 1]) for i in range(2)]
        rs1 = [sb("rs1_%d" % i, [128, 1]) for i in range(2)]
        xnb = [sb("xnb%d" % i, [128, D], BF16) for i in range(2)]
        stt = xnb[1]
        tA = [sb("tA%d" % i, [128, 512]) for i in range(2)]
        ssq = [sb("ssq%d" % i, [128, 8]) for i in range(2)]
        rq = [sb("rq%d" % i, [128, 8]) for i in range(2)]
        sqn = sb("sqn", [128, 512])
        qcp = [sb("qcp%d" % i, [128, 512]) for i in range(2)]
        tB, tC = [qcp[0]], [qcp[1]]
        bevb = [sb("bev%d" % i, [128, 512]) for i in range(1)]
        xh = [sb("xh%d" % i, [128, 512]) for i in range(3)]
        qtab = [sb("qtab%d" % i, [128, NU, 32]) for i in range(4)]
        r1 = [sb("r1_%d" % i, [128, 256]) for i in range(1)]
        r2 = [sb("r2_%d" % i, [128, 256]) for i in range(1)]
        kf = [sb("kf%d" % i, [128, 256]) for i in range(1)]
        vf = [sb("vf%d" % i, [128, 256]) for i in range(1)]
        kdup = [sb("kdup%d" % i, [128, 4, 2, 64], BF16) for i in range(G)]
        PT = [sb("PT%d" % i, [128, 4, 128], BF16) for i in range(4)]
        dsum = [sb("dsum%d" % i, [128, 16]) for i in range(2)]
        rden = [sb("rden%d" % i, [128, 16]) for i in range(2)]
        ckb = [Buf("ckb%d" % i, xbuf[0].t[:, i * 256:(i + 1) * 256]) for i in range(4)]
        cvb = [Buf("cvb%d" % i, xh[i // 2].t[:, (i % 2) * 256:(i % 2 + 1) * 256]) for i in range(4)]
        KTc = [Buf("KTc%d" % i, xbuf[1].t[:, i * 512:(i + 1) * 512].bitcast(BF16).rearrange("p (g v k) -> p g v k", g=4, v=2))
               for i in range(2)]
        V65c = [Buf("V65c%d" % i, xnb[0].t[:, i * 512:i * 512 + 260].rearrange("p (g e) -> p g e", e=65)) for i in range(2)]

        ctr = {}
        deferred = []
        mmc = [0]

        def defer(fn, lag=1):
            deferred.append((mmc[0] + lag, fn))

        def flush(force=False):
            progress = True
            while progress:
                progress = False
                for it in list(deferred):
                    if force or it[0] <= mmc[0]:
                        deferred.remove(it)
                        it[1]()
                        progress = True
                        break

        def rot(lst, key):
            i = ctr.get(key, 0)
            ctr[key] = i + 1
            return lst[i % len(lst)]

        cst = S.new_stream("const")

        def cdma(dst, src_ap, dst_ap=None):
            S.dma(lambda e: e.dma_start(out=(dst[:] if dst_ap is None else dst_ap), in_=src_ap),
                  writes=[dst], stream=cst)

        cdma(xbuf[0], ident_d, xbuf[0][:, 0:128])
        cdma(cs, cosr.rearrange("(u p) f -> p u f", p=128))
        cdma(sn, sinr.rearrange("(u p) f -> p u f", p=128))
        cdma(xbuf[1], shifts_d.rearrange("m p f -> p m f"), xbuf[1][:, 0:1024].rearrange("p (m f) -> p m f", f=128))
        for i in range(3):
            cdma(wbc[i], conv_w[i:i + 1, :].broadcast_to([128, D]))
        cdma(wq_bc, qnw.unsqueeze(0).broadcast_to([128, 64]))
        cdma(wk_bc, knw.unsqueeze(0).broadcast_to([128, 64]))
        cdma(esink, sinks.unsqueeze(0).broadcast_to([128, 16]))
        S.dma(lambda e: e.dma_start(out=nw[:], in_=norm_w.rearrange("(k p) -> p k", p=128),
                                    allow_slow_non_contiguous=True), writes=[nw], stream=cst)
        MSPL = [(0, 7), (7, 7), (14, 6)]
        for i, (m0_, mn_) in enumerate(MSPL):
            cdma(stg[i], masks_d[m0_:m0_ + mn_].rearrange("m p f -> p m f"),
                 stg[i][:, :, :].rearrange("p a b -> p (a b)")[:, 0:mn_ * 128].rearrange("p (m f) -> p m f", f=128))
        S.fence([xbuf[0], xbuf[1], cs, sn, wbc[0], wbc[1], wbc[2], wq_bc, wk_bc, esink, nw] + stg)
        S.op("dve", lambda: V.tensor_copy(out=ident[:], in_=xbuf[0][:, 0:128]), reads=[xbuf[0]], writes=[ident])
        S.op("dve", lambda: V.tensor_copy(out=shifts[:].rearrange("p m f -> p (m f)"), in_=xbuf[1][:, 0:1024]),
             reads=[xbuf[1]], writes=[shifts])
        for i, (m0_, mn_) in enumerate(MSPL):
            S.op("pool", lambda i=i, m0_=m0_, mn_=mn_: P.tensor_copy(
                out=masks[:, m0_:m0_ + mn_, :].rearrange("p m f -> p (m f)"),
                in_=stg[i][:, :, :].rearrange("p a b -> p (a b)")[:, 0:mn_ * 128]), reads=[stg[i]], writes=[masks])
        S.op("pool", lambda: P.memset(ucarry[:], 0.0), writes=[ucarry])
        S.op("act", lambda: A.activation(out=esink[:], in_=esink[:], func=AF.Exp), reads=[esink], writes=[esink])
        for ti, (tb_, lo) in enumerate([(cs, 0), (sn, 32), (cs, 32), (sn, 0)]):
            S.op("dve", lambda ti=ti, tb_=tb_, lo=lo: V.tensor_tensor(
                out=qtab[ti][:], in0=tb_[:], in1=wq_bc[:, lo:lo + 32].unsqueeze(1).broadcast_to([128, NU, 32]),
                op=ALU.mult), reads=[tb_, wq_bc], writes=[qtab[ti]])
        for kt_ in KT2:
            S.op("pool", lambda kt_=kt_: P.memset(kt_[:], 0.0), writes=[kt_])
        for i in range(G):
            S.op("pool", lambda i=i: P.memset(V65[i][:, :, 64:65], 1.0), writes=[V65[i]])

        def piece_dma_half(src, col0, half):
            w = wsrc[src]
            sg = rot(stg, "stg")
            S.dma(lambda e: e.dma_start(
                out=sg[:], in_=w[half * 512:(half + 1) * 512, col0:col0 + 512].rearrange("(k p) n -> p k n", p=128)),
                writes=[sg])
            return sg

        def piece_cast(src, hs):
            wp = rot(wpc, "wpc")
            opsl = []
            for half in range(2):
                sg = hs[half]
                if src == "win":
                    for kk in range(4):
                        k = half * 4 + kk
                        if False:
                            opsl.append(lambda sg=sg, kk=kk, k=k: S.op("pool", lambda: P.tensor_scalar(
                                out=wp[:, k, :], in0=sg[:, kk, :], scalar1=nw[:, k:k + 1], scalar2=1.0,
                                op0=ALU.mult, op1=ALU.mult), reads=[sg, nw], writes=[wp]))
                        else:
                            opsl.append(lambda sg=sg, kk=kk, k=k: S.op("act", lambda: A.activation(
                                out=wp[:, k, :], in_=sg[:, kk, :], func=AF.Copy, scale=nw[:, k:k + 1]),
                                reads=[sg, nw], writes=[wp]))
                else:
                    for q2 in range(2):
                        opsl.append(lambda sg=sg, half=half, q2=q2: S.op("dve", lambda: V.tensor_copy(
                            out=wp[:, half * 4 + q2 * 2:half * 4 + q2 * 2 + 2, :], in_=sg[:, q2 * 2:q2 * 2 + 2, :]),
                            reads=[sg], writes=[wp]))
            return wp, opsl

        def kind(u):
            return "halo" if u == 0 else ("sample" if u == 17 else "prompt")

        def transposes(src_fn, dstT, srcbufs, n=8, dst_sl=None):
            tb, tvw = tp_pick()
            for k in range(n):
                S.op("pe", lambda k=k: T.transpose(out=tvw[:, k, :], in_=src_fn(k), identity=ident[:]),
                     reads=srcbufs + [ident], writes=[tb])
            psum_copy(dstT[:] if dst_sl is None else dst_sl, tvw[:, 0:n, :], [tb], [dstT])

        def stage_x_load(u):
            xb = rot(xbuf, "xbuf")
            S.dma(lambda e: e.dma_start(out=xb[:], in_=xin[u * 128:(u + 1) * 128, :]), writes=[xb])
            return xb

        def stage_x_p1(u, xb):
            ss = rot(ss1, "ss1")
            rs = rot(rs1, "rs1")
            xn_ = rot(xnb, "xnb")
            S.op("act", lambda: A.activation(out=xn_[:], in_=xb[:], func=AF.Square, accum_out=ss[:]),
                 reads=[xb], writes=[xn_, ss])
            S.op("act", lambda: A.activation(out=rs[:], in_=ss[:], func=AF.Sqrt, bias=EPS, scale=1.0 / D),
                 reads=[ss], writes=[rs])
            return rs, xn_

        def stage_x_p2(u, xb, rs, xn_):
            S.op("dve", lambda: V.reciprocal(out=rs[:], in_=rs[:]), reads=[rs], writes=[rs])
            S.op("act", lambda: A.activation(out=xn_[:], in_=xb[:], func=AF.Copy, scale=rs[:, 0:1]),
                 reads=[xb, rs], writes=[xn_])

        def stage_x_front(u, xb):
            rs, xn_ = stage_x_p1(u, xb)
            stage_x_p2(u, xb, rs, xn_)
            return xn_

        def xpre_trigger(un):
            xb = stage_x_load(un)

            def p1():
                rs, xn_ = stage_x_p1(un, xb)

                def p2():
                    stage_x_p2(un, xb, rs, xn_)
                    defer(lambda: stage_x_back(un, xn_), lag=2)
                defer(p2, lag=1)
            defer(p1, lag=2)

        def stage_x_back(u, xn_):
            transposes(lambda k: xn_[:, k * 128:(k + 1) * 128], xnT[u % G], [xn_])

        def stage_x(u):
            stage_x_back(u, stage_x_front(u, stage_x_load(u)))

        acc3 = acc + [ob[2]]
        acc_wide = [False]

        def main_mm(u, lhsT, wpair):
            wpb, wpv = wpair
            a = rot(acc3, "acc3") if acc_wide[0] else rot(acc, "acc")
            for k in range(8):
                S.op("pe", lambda k=k: T.matmul(a[:], lhsT=lhsT[:, k, :], rhs=wpv[:, k, :], start=(k == 0), stop=(k == 7)),
                     reads=[lhsT, wpb], writes=[a])
            return a

        def ep_c(u, hf, a):
            f = FH[hf][u % G]
            S.op("act", lambda: A.copy(out=f[:], in_=a[:]), reads=[a], writes=[f])

        def ep_h(u, hf, a):
            s = u % G
            f = FH[hf][s]
            bh = BH[hf][s]
            S.op("dve", lambda: V.tensor_tensor(out=f[:], in0=f[:], in1=a[:], op=ALU.mult), reads=[f, a], writes=[f])
            S.op("act", lambda: A.copy(out=bh[:], in_=f[:]), reads=[f], writes=[bh])
            if u == 16:
                S.dma(lambda e: e.dma_start(out=ncp_d[:, hf * 512:(hf + 1) * 512], in_=f[126:128, :]), reads=[f])
            if u == 17:
                for sq_ in range(NSEQ):
                    S.dma(lambda e, sq_=sq_: e.dma_start(out=ncs_d[sq_, :, hf * 512:(hf + 1) * 512],
                                                         in_=f[sq_ * 8 + 6:sq_ * 8 + 8, :]), reads=[f])

        shsel = {}

        def flat(b_):
            return b_[:].rearrange("p a b -> p (a b)") if len(b_[:].shape) == 3 else b_[:]

        def pre_b(u, hf):
            s = u % G
            ub = BH[hf][s]
            smp = (u == 17)
            m0 = 4 if smp else 0
            if smp:
                prev, prev_ap = stt, stt[:, hf * 512:(hf + 1) * 512]
            elif s == 0:
                prev, prev_ap = ucarry, ucarry[:, hf * 512:(hf + 1) * 512]
            else:
                prev, prev_ap = BH[hf][s - 1], BH[hf][s - 1][:]
            par = ctr.get("shpar", 0)
            ctr["shpar"] = par + 1
            sx, sy = (stb[0], stb[1]) if par % 2 == 0 else (ob[0], ob[1])
            shsel[(u, hf)] = (sx, sy)
            S.op("pe", lambda: T.matmul(flat(sx), lhsT=shifts[:, m0 + 0, :], rhs=ub[:],
                                        start=True, stop=False), reads=[shifts, ub], writes=[sx])
            S.op("pe", lambda: T.matmul(flat(sx), lhsT=shifts[:, m0 + 2, :], rhs=prev_ap,
                                        start=False, stop=True), reads=[shifts, prev], writes=[sx])
            S.op("pe", lambda: T.matmul(flat(sy), lhsT=shifts[:, m0 + 1, :], rhs=ub[:],
                                        start=True, stop=False), reads=[shifts, ub], writes=[sy])
            S.op("pe", lambda: T.matmul(flat(sy), lhsT=shifts[:, m0 + 3, :], rhs=prev_ap,
                                        start=False, stop=True), reads=[shifts, prev], writes=[sy])

        def ep_b(u, hf, a):
            s = u % G
            f = FH[hf][s]
            sx, sy = shsel.pop((u, hf))
            SX, SY = [sx], [sy]
            bev = rot(bevb, "bev")
            S.op("act", lambda: A.copy(out=bev[:], in_=a[:]), reads=[a], writes=[bev])
            t1 = rot(tA, "tA")
            t2 = rot(tB, "tB")
            t3 = rot(tC, "tC")
            cs_ = slice(hf * 512, (hf + 1) * 512)
            S.op("dve", lambda: V.tensor_tensor(out=t1[:], in0=flat(sx), in1=wbc[1][:, cs_],
                                                op=ALU.mult), reads=SX + [wbc[1]], writes=[t1])
            S.op("dve", lambda: V.tensor_tensor(out=t2[:], in0=flat(sy), in1=wbc[0][:, cs_],
                                                op=ALU.mult), reads=SY + [wbc[0]], writes=[t2])
            S.op("pool", lambda: P.tensor_tensor(out=t3[:], in0=f[:], in1=wbc[2][:, cs_], op=ALU.mult),
                 reads=[f, wbc[2]], writes=[t3])
            S.op("dve", lambda: V.tensor_tensor(out=t1[:], in0=t1[:], in1=t2[:], op=ALU.add), reads=[t1, t2], writes=[t1])
            S.op("pool", lambda: P.tensor_tensor(out=t1[:], in0=t1[:], in1=t3[:], op=ALU.add), reads=[t1, t3], writes=[t1])
            S.op("dve", lambda: V.tensor_tensor(out=f[:], in0=t1[:], in1=bev[:], op=ALU.mult), reads=[t1, bev], writes=[f])

        def ep_zb(u, hf, a):
            s = u % G
            f = FH[hf][s]
            bh = BH[hf][s]
            t1 = rot(tA, "tA")
            S.op("act", lambda: A.activation(out=t1[:], in_=a[:], func=AF.Silu), reads=[a], writes=[t1])
            S.op("pool", lambda: P.tensor_tensor(out=bh[:], in0=f[:], in1=t1[:], op=ALU.mult), reads=[f, t1], writes=[bh])

        def half_transposes(u, dstT):
            s = u % G
            tb, tvw = tp_pick()
            for hf in range(2):
                bh = BH[hf][s]
                for kk in range(4):
                    k = hf * 4 + kk
                    S.op("pe", lambda k=k, kk=kk, bh=bh: T.transpose(out=tvw[:, k, :], in_=bh[:, kk * 128:(kk + 1) * 128],
                                                                     identity=ident[:]), reads=[bh, ident], writes=[tb])
            psum_copy(dstT[:], tvw[:, :, :], [tb], [dstT])

        def norm_rope(u, a_ap, a_buf, nh, w_bc, out_ap_fn, out_bufs):
            sq_ = sqn
            ss = rot(ssq, "ssq")
            r = rot(rq, "rq")
            q_ = rot(qcp, "qcp")
            w = nh * 64
            S.op("act", lambda: A.activation(out=sq_[:, 0:w], in_=a_ap, func=AF.Square), reads=[a_buf], writes=[sq_])
            S.op("act", lambda: A.copy(out=q_[:, 0:w], in_=a_ap), reads=[a_buf], writes=[q_])
            S.op("dve", lambda: V.tensor_reduce(out=ss[:, 0:nh], in_=sq_[:, 0:w].rearrange("p (h d) -> p h d", d=64),
                                                axis=AX.X, op=ALU.add), reads=[sq_], writes=[ss])
            S.op("act", lambda: A.activation(out=r[:, 0:nh], in_=ss[:, 0:nh], func=AF.Sqrt, bias=EPS, scale=1.0 / 64),
                 reads=[ss], writes=[r])
            S.op("dve", lambda: V.reciprocal(out=r[:, 0:nh], in_=r[:, 0:nh]), reads=[r], writes=[r])
            S.op("dve", lambda: V.tensor_tensor(
                out=q_[:, 0:w].rearrange("p (h d) -> p h d", d=64), in0=q_[:, 0:w].rearrange("p (h d) -> p h d", d=64),
                in1=r[:, 0:nh].unsqueeze(2).broadcast_to([128, nh, 64]), op=ALU.mult), reads=[q_, r], writes=[q_])
            if w_bc is not None:
                S.op("pool", lambda: P.tensor_tensor(
                    out=q_[:, 0:w].rearrange("p (h d) -> p h d", d=64), in0=q_[:, 0:w].rearrange("p (h d) -> p h d", d=64),
                    in1=w_bc[:, :].unsqueeze(1).broadcast_to([128, nh, 64]), op=ALU.mult), reads=[q_, w_bc], writes=[q_])
            qv = q_[:, 0:w].rearrange("p (h t d) -> p h t d", t=2, d=32)
            x1 = qv[:, :, 0, :]
            x2 = qv[:, :, 1, :]
            if w_bc is not None:
                tbs = [cs, sn, cs, sn]
            else:
                tbs = qtab
            tv = [t_[:, u, :].unsqueeze(1).broadcast_to([128, nh, 32]) for t_ in tbs]
            cb, sb_, cb2, sb2 = tv[0], tv[1], tv[2], tv[3]
            cs_b, sn_b, cs_b2, sn_b2 = tbs
            a1, a2 = r1[0], r2[0]
            a3, a4 = a1, a2

            def v3(t):
                return t[:, 0:nh * 32].rearrange("p (h d) -> p h d", d=32)
            S.op("dve", lambda: V.tensor_tensor(out=v3(a1), in0=x1, in1=cb, op=ALU.mult), reads=[q_, cs_b], writes=[a1])
            S.op("pool", lambda: P.tensor_tensor(out=v3(a2), in0=x2, in1=sb_, op=ALU.mult), reads=[q_, sn_b], writes=[a2])
            S.op("dve", lambda: V.tensor_tensor(out=out_ap_fn(0), in0=v3(a1), in1=v3(a2), op=ALU.subtract),
                 reads=[a1, a2], writes=out_bufs)
            S.op("dve", lambda: V.tensor_tensor(out=v3(a3), in0=x2, in1=cb2, op=ALU.mult), reads=[q_, cs_b2], writes=[a3])
            S.op("pool", lambda: P.tensor_tensor(out=v3(a4), in0=x1, in1=sb2, op=ALU.mult), reads=[q_, sn_b2], writes=[a4])
            S.op("pool", lambda: P.tensor_tensor(out=out_ap_fn(1), in0=v3(a3), in1=v3(a4), op=ALU.add),
                 reads=[a3, a4], writes=out_bufs)

        def ep_q(u, hf, a):
            s = u % G
            bh = BH[hf][s]
            bv = bh[:].rearrange("p (h t d) -> p h t d", t=2, d=32)
            norm_rope(u, a[:], a, 8, None, lambda part: bv[:, :, part, :], [bh])

        def q_transposes(u):
            s = u % G
            half_transposes(u, gaT[s])

        def kt_transposes(kd, ktdst):
            for g in range(4):
                S.op("pe", lambda g=g: T.transpose(out=tp[:, g, :], in_=kd[:, g, :, :].rearrange("p a d -> p (a d)"),
                                                   identity=ident[:]), reads=[kd, ident], writes=[tp])
            S.op("dve", lambda: V.tensor_copy(out=ktdst[0:64, :, 0, :], in_=tp[0:64, 0:4, :]), reads=[tp], writes=[ktdst])
            S.op("dve", lambda: V.tensor_copy(out=ktdst[64:128, :, 1, :], in_=tp[64:128, 0:4, :]), reads=[tp], writes=[ktdst])

        def cast_kv(kf_, vf_, kd, v65dst):
            S.op("act", lambda: A.copy(out=kd[:], in_=kf_[:].rearrange("p (g d) -> p g d", d=64).unsqueeze(2).broadcast_to([128, 4, 2, 64])),
                 reads=[kf_], writes=[kd])
            S.op("dve", lambda: V.tensor_copy(out=v65dst[:, :, 0:64], in_=vf_[:].rearrange("p (g d) -> p g d", d=64)),
                 reads=[vf_], writes=[v65dst])

        def ep_kv(u, a):
            s = u % G
            kd = kdup[s]
            if u in (16, 17):
                kf_, vf_ = kf[0], vf[0]
                kv3 = kf_[:].rearrange("p (h t d) -> p h t d", t=2, d=32)
                norm_rope(u, a[:, 0:256], a, 4, wk_bc, lambda part: kv3[:, :, part, :], [kf_])
                S.op("act", lambda: A.copy(out=vf_[:], in_=a[:, 256:512]), reads=[a], writes=[vf_])
                cast_kv(kf_, vf_, kd, V65[s])
                if u == 16:
                    S.dma(lambda e: e.dma_start(out=nkp_d, in_=kf_[:]), reads=[kf_])
                    S.dma(lambda e: e.dma_start(out=nvp_d, in_=vf_[:]), reads=[vf_])
                else:
                    for sq_ in range(NSEQ):
                        S.dma(lambda e, sq_=sq_: e.dma_start(out=nks_d[sq_, 120:128, :], in_=kf_[sq_ * 8:(sq_ + 1) * 8, :]),
                              reads=[kf_])
                        S.dma(lambda e, sq_=sq_: e.dma_start(out=nvs_d[sq_, 120:128, :], in_=vf_[sq_ * 8:(sq_ + 1) * 8, :]),
                              reads=[vf_])
            else:
                kd0 = kd[:, :, 0, :].rearrange("p g (t d) -> p g t d", t=2)
                norm_rope(u, a[:, 0:256], a, 4, wk_bc, lambda part: kd0[:, :, part, :], [kd])
                S.op("pool", lambda: P.tensor_copy(out=kd[:, :, 1, :], in_=kd[:, :, 0, :]), reads=[kd], writes=[kd])
                S.op("dve", lambda: V.tensor_copy(out=V65[s][:, :, 0:64], in_=a[:, 256:512].rearrange("p (g d) -> p g d", d=64)),
                     reads=[a], writes=[V65[s]])

        def attention(u, keyblocks, loaders=None):
            s = u % G
            QT = gaT[s]
            nkb = len(keyblocks)
            items = [(kb, g) for kb in range(nkb) for g in range(4)]
            pts = {}
            kbd = {}

            def need(kb):
                if kb < nkb and kb not in kbd:
                    kbd[kb] = keyblocks[kb]()

            def scores(i):
                kb, g = items[i]
                if loaders is not None and g == 0:
                    for kb2 in range(kb, min(kb + 4, len(loaders))):
                        loaders[kb2]()
                need(kb)
                if g == 3:
                    need(kb + 1)
                KT, _, mi = kbd[kb]
                stt_, stv = rot(stq, "stq")
                pt = rot(PT, "PT")
                pts[i] = pt
                S.op("pe", lambda: T.matmul(stv, lhsT=ident[:],
                                            rhs=masks[:, mi, :].unsqueeze(1).broadcast_to([128, 4, 128]),
                                            start=True, stop=False), reads=[ident, masks], writes=[stt_])
                for var in range(2):
                    S.op("pe", lambda var=var: T.matmul(
                        stv[:, 2 * var:2 * var + 2, :], lhsT=KT[:, g, var, :],
                        rhs=QT[:, 2 * g:2 * g + 2, :], start=False, stop=(var == 1)),
                        reads=[KT, QT], writes=[stt_])
                if "exp" in skip:
                    return
                S.op("act", lambda: A.activation(out=pt[:].rearrange("p a b -> p (a b)"),
                                                 in_=stv.rearrange("p a b -> p (a b)"), func=AF.Exp, scale=0.125),
                     reads=[stt_], writes=[pt])
                pts[i] = pt

            def bank_of(h):
                return 0 if h < 6 else (1 if h < 12 else 2)

            def head_of(g, j):
                return 4 * g + 2 * (j % 2) + (j // 2)
            first_w, last_w = {}, {}
            for i_, (kb_, g_) in enumerate(items):
                for j_ in range(4):
                    bk_ = bank_of(head_of(g_, j_))
                    first_w.setdefault(bk_, (i_, j_))
                    last_w[bk_] = (i_, j_)

            def pv(i):
                kb, g = items[i]
                _, VV, _ = kbd[kb]
                pt = pts.pop(i)
                for j in range(4):
                    if "pv" in skip:
                        break
                    h = head_of(g, j)
                    bk = bank_of(h)
                    obuf, oap = o_head(h)
                    S.op("pe", lambda j=j, oap=oap, bk=bk: T.matmul(oap, lhsT=pt[:, j, :], rhs=VV[:, g, :],
                                                                    start=(first_w[bk] == (i, j)),
                                                                    stop=(last_w[bk] == (i, j))),
                         reads=[pt, VV], writes=[obuf])

            n = len(items)
            LAG = 3
            for i in range(n + LAG):
                if i < n:
                    scores(i)
                if i - LAG >= 0:
                    pv(i - LAG)
            if "onorm" in skip:
                return
            ds = rot(dsum, "dsum")
            rd = rot(rden, "rden")
            for bk, (h0, nh) in enumerate(HB):
                S.op("dve", lambda bk=bk, h0=h0, nh=nh: V.tensor_tensor(
                    out=ds[:, h0:h0 + nh], in0=ob[bk][:, 0:nh * 65].rearrange("p (h e) -> p h e", e=65)[:, :, 64],
                    in1=esink[:, h0:h0 + nh], op=ALU.add), reads=[ob[bk], esink], writes=[ds])
            S.op("dve", lambda: V.reciprocal(out=rd[:], in_=ds[:]), reads=[ds], writes=[rd])
            segs = [(0, 0, 6, F32a[s], 0), (1, 0, 2, F32a[s], 384), (1, 2, 4, F32b[s], 0), (2, 0, 4, F32b[s], 256)]
            for (bk, sl0, nh, dst, c0) in segs:
                h0 = HB[bk][0] + sl0
                S.op("dve", lambda bk=bk, sl0=sl0, nh=nh, dst=dst, c0=c0, h0=h0: V.tensor_tensor(
                    out=dst[:, c0:c0 + nh * 64].rearrange("p (h d) -> p h d", d=64),
                    in0=ob[bk][:, sl0 * 65:(sl0 + nh) * 65].rearrange("p (h e) -> p h e", e=65)[:, :, 0:64],
                    in1=rd[:, h0:h0 + nh].unsqueeze(2).broadcast_to([128, nh, 64]), op=ALU.mult),
                    reads=[ob[bk], rd], writes=[dst])

        cache_ld = {}
        cache_issued = set()

        def cache_loader(sq_):
            def ld():
                if sq_ in cache_issued:
                    return
                cache_issued.add(sq_)
                ck = rot(ckb, "ckb")
                cv = rot(cvb, "cvb")
                S.dma(lambda e: e.dma_start(out=ck[:], in_=cache_k[sq_]), writes=[ck])
                S.dma(lambda e: e.dma_start(out=cv[:], in_=cache_v[sq_]), writes=[cv])
                cache_ld[sq_] = (ck, cv)
            return ld

        def cache_provider(sq_):
            def prov():
                cache_loader(sq_)()
                ck, cv = cache_ld.pop(sq_)
                kt = rot(KTc, "KTc")
                vv = rot(V65c, "V65c")
                kd = rot(kdup[0:2], "kdc")
                cast_kv(ck, cv, kd, vv)
                kt_transposes(kd, kt)
                return (kt, vv, 4 + sq_)
            return prov

        def ep_za(u, hf, a):
            s = u % G
            f = FH[hf][s]
            bh = BH[hf][s]
            t1 = rot(tA, "tA")
            S.op("act", lambda: A.activation(out=t1[:], in_=a[:], func=AF.Silu), reads=[a], writes=[t1])
            S.op("pool", lambda: P.tensor_tensor(out=bh[:], in0=f[:], in1=t1[:], op=ALU.mult), reads=[f, t1], writes=[bh])

        def ep_ga(u, hf, a):
            f = F32a[u % G]
            S.op("act", lambda: A.activation(out=f[:], in_=a[:], func=AF.Sigmoid), reads=[a], writes=[f])

        def ep_pa(u, hf, a):
            f = F32a[u % G]
            S.op("dve", lambda: V.tensor_tensor(out=f[:], in0=f[:], in1=a[:], op=ALU.mult), reads=[f, a], writes=[f])

        def ep_gb(u, hf, a):
            f = F32b[u % G]
            S.op("act", lambda: A.activation(out=f[:], in_=a[:], func=AF.Sigmoid), reads=[a], writes=[f])

        def ep_pb(u, hf, a):
            s = u % G
            t1 = rot(tA, "tA")
            bh = BH[hf][s]
            S.op("dve", lambda: V.tensor_tensor(out=t1[:], in0=F32b[s][:], in1=a[:], op=ALU.mult),
                 reads=[F32b[s], a], writes=[t1])
            S.op("pool", lambda: P.tensor_tensor(out=bh[:], in0=t1[:], in1=F32a[s][:], op=ALU.add),
                 reads=[t1, F32a[s]], writes=[bh])

        wo_buf = {}
        wo_queue = []
        wo_lagq = []

        def wo_load(u, hf):
            xb = rot(xh, "xh")
            wo_buf[(u, hf)] = xb
            S.dma(lambda e: e.dma_start(out=xb[:], in_=xin[u * 128:(u + 1) * 128, hf * 512:(hf + 1) * 512]),
                  writes=[xb])

        def ep_wo(u, hf, a):
            xb = wo_buf.pop((u, hf))
            S.op("dve", lambda: V.tensor_tensor(out=xb[:], in0=a[:], in1=xb[:], op=ALU.add), reads=[a, xb], writes=[xb])
            S.dma(lambda e: e.dma_start(out=y_d[(u - 1) * 128:u * 128, hf * 512:(hf + 1) * 512], in_=xb[:]),
                  reads=[xb])

        dd = S.new_stream("d2d")
        if "d2d" not in skip:
            S.dma(lambda e: e.dma_start(out=nks_d[:, 0:120, :], in_=cache_k[:, 8:128, :]), stream=dd)
            S.dma(lambda e: e.dma_start(out=nvs_d[:, 0:120, :], in_=cache_v[:, 8:128, :]), stream=dd)

        xpre_buf = {}
        XPRE = {"pb1": 0, "wo0": 2, "wo1": 4}
        allst = [(p, si) for p in range(3) for si in range(len(STAGES))]
        pend = {}

        def issue_dma_half(idx, half):
            if idx < len(allst):
                _, si = allst[idx]
                pend.setdefault(idx, []).append(piece_dma_half(STAGES[si][1], STAGES[si][2], half))

        def start_cast(idx):
            if idx >= len(allst):
                return None, []
            _, si = allst[idx]
            return piece_cast(STAGES[si][1], pend.pop(idx))

        NFP = NPIECE + 2
        wsc_all = Buf("wsc_all")
        wsc_stream = S.new_stream("wsc")

        def scratch_write(n, wpb, src):
            if n >= NPIECE:
                return
            q_ = "act"
            S.dma(lambda e: e.dma_start(out=wsc[n], in_=wpb[:].rearrange("p k n -> p (k n)")),
                  reads=[wpb], writes=[wsc_all], queue=q_, stream=wsc_stream)

        def bf_view(b_):
            if b_ in wpc:
                return b_.t[:]
            return b_.t[:].bitcast(BF16).rearrange("p a (b c) -> p (a b) c", c=512)

        ring = [stg[(2 * NFP - 1) % 3], stg[(2 * NFP - 2) % 3 - 0], None, None, None]
        h0_, h1_ = (2 * (NFP - 1)) % 3, (2 * (NFP - 1) + 1) % 3
        free_ = [i for i in range(3) if i not in (h0_, h1_)][0]
        ring = [stg[free_], stg[h0_], stg[h1_], wpc[(NFP - 2) % 2], wpc[(NFP - 1) % 2]]
        slot_free_at = {0: NFP - 2, 1: NFP - 1, 2: NFP - 1, 3: NFP - 1, 4: NFP}
        ring_piece = {}
        next_load = [NFP]

        def ring_loads(idx):
            while next_load[0] < len(allst) and next_load[0] <= idx + 3:
                n = next_load[0]
                k_ = (n - NFP) % 5
                if slot_free_at[k_] > idx:
                    break
                b_ = ring[k_]
                v_ = bf_view(b_)
                S.dma(lambda e, v_=v_, n=n: e.dma_start(out=v_.rearrange("p k n -> p (k n)"), in_=wsc[n % NPIECE]),
                      reads=[wsc_all], writes=[b_])
                ring_piece[n] = (b_, v_)
                slot_free_at[k_] = n + 1
                next_load[0] += 1

        issue_dma_half(0, 0)
        issue_dma_half(0, 1)
        wp0, ops0 = start_cast(0)
        wp_cur = (wp0, wp0.t[:])
        for f_ in ops0:
            f_()
        scratch_write(0, wp0, STAGES[0][1])
        issue_dma_half(1, 0)
        issue_dma_half(1, 1)
        for idx, (p, si) in enumerate(allst):
            if limit is not None and idx >= limit:
                break
            name = STAGES[si][0]
            units = list(range(p * G, (p + 1) * G))
            if si == 0 and p == 0:
                for u in units:
                    stage_x(u)
            wp = wp_cur
            ring_loads(idx)
            if idx + 2 < NFP:
                issue_dma_half(idx + 2, 0)
            if idx + 1 < NFP:
                wpn_, cast_ops = start_cast(idx + 1)
                wp_next = (wpn_, wpn_.t[:]) if wpn_ is not None else None
            else:
                cast_ops = []
                wp_next = ring_piece.pop(idx + 1, None)
            nsrc_ = STAGES[allst[idx + 1][1]][1] if idx + 1 < len(allst) else None
            scratch_done = [idx + 1 >= NFP or not cast_ops]
            n_cast = len(cast_ops)
            n_emitted = 0
            second_half_issued = False
            n_units_here = sum(1 for u_ in units if not (kind(u_) == "halo" and name not in HALO_STAGES))
            base = name.rstrip("01")
            hf = 1 if name.endswith("1") else 0
            if p == 2 and name == "h1":
                for hh in range(2):
                    S.dma(lambda e, hh=hh: e.dma_start(out=tA[hh][0:32, :], in_=state_conv[:, hh * 512:(hh + 1) * 512]),
                          writes=[tA[hh]])
                S.op("pool", lambda: P.memset(stt[:], 0.0), writes=[stt])
                for hh in range(2):
                    S.op("pool", lambda hh=hh: P.tensor_copy(out=stt[0:32, hh * 512:(hh + 1) * 512], in_=tA[hh][0:32, :]),
                         reads=[tA[hh]], writes=[stt])
            st_free[0] = name not in ("b0", "b1", "kv", "za0")
            acc_wide[0] = name not in ("kv", "za0", "za1")
            jproc = 0
            proc_units = [u_ for u_ in units if not (kind(u_) == "halo" and name not in HALO_STAGES)]
            if name == "pb1":
                wo_queue.clear()
                wo_queue.extend([(u_, 0) for u_ in proc_units] + [(u_, 1) for u_ in proc_units])
                for _ in range(len(xh)):
                    wo_load(*wo_queue.pop(0))
            for u in units:
                s = u % G
                kd = kind(u)
                if kd == "halo" and name not in HALO_STAGES:
                    continue
                if base in ("pa",):
                    lhs = gaT[s]
                elif base in ("pb",):
                    lhs = gbT[s]
                elif base == "wo":
                    lhs = gbT[s]
                else:
                    lhs = xnT[s]
                if base == "b":
                    pre_b(u, hf)
                a = main_mm(u, lhs, wp)
                mmc[0] += 1
                flush()
                jproc += 1
                if p < 2 and name in XPRE and jproc in (1, 3):
                    xpre_trigger((p + 1) * G + XPRE[name] + (1 if jproc == 3 else 0))
                if base == "c":
                    ep_c(u, hf, a)
                elif base == "h":
                    ep_h(u, hf, a)
                elif base == "b":
                    ep_b(u, hf, a)
                    if hf == 1 and s == G - 1 and u != 17:
                        S.op("pool", lambda s=s: P.tensor_copy(out=ucarry[:, 0:512], in_=BFa[s][:]),
                             reads=[BFa[s]], writes=[ucarry])
                        S.op("pool", lambda s=s: P.tensor_copy(out=ucarry[:, 512:1024], in_=BFb[s][:]),
                             reads=[BFb[s]], writes=[ucarry])
                elif base == "zb":
                    ep_zb(u, hf, a)
                    if hf == 1:
                        defer(lambda u=u, s=s: half_transposes(u, gbT[s]), lag=2)
                elif base == "q":
                    ep_q(u, hf, a)
                    if hf == 1:
                        defer(lambda u=u: q_transposes(u), lag=3)
                elif name == "kv":
                    ep_kv(u, a)

                    jk = proc_units.index(u)
                    nxt_u = proc_units[jk + 1] if jk + 1 < len(proc_units) else None

                    def kv_aux(u=u, s=s, kd=kd, jk=jk, nxt_u=nxt_u):
                        if jk == 0:
                            kt_transposes(kdup[s], KT2[s])
                        if "attn" in skip:
                            pass
                        elif kd == "prompt":
                            ps_ = (u - 1) % G
                            mprev = 2 if u == 1 else 0
                            attention(u, [lambda: (KT2[ps_], V65[ps_], mprev),
                                          lambda: (KT2[s], V65[s], 1)])
                        elif kd == "sample":
                            S.barrier([xbuf[0], xbuf[1], xnb[0], xh[0], xh[1]])
                            for kt_ in KTc:
                                S.op("pool", lambda kt_=kt_: P.memset(kt_[:], 0.0), writes=[kt_])
                            for vv_ in V65c:
                                S.op("pool", lambda vv_=vv_: P.memset(vv_[:, :, 64:65], 1.0), writes=[vv_])
                            kbs = [cache_provider(sq_) for sq_ in range(NSEQ)]
                            kbs.append(lambda: (KT2[s], V65[s], 3))
                            attention(u, kbs, loaders=[cache_loader(sq_) for sq_ in range(NSEQ)])
                            S.barrier(ckb + cvb + KTc + V65c)
                        if nxt_u is not None:
                            kt_transposes(kdup[nxt_u % G], KT2[nxt_u % G])
                    defer(kv_aux, lag=4)
                elif base == "za":
                    ep_za(u, hf, a)
                    if hf == 1:
                        defer(lambda u=u, s=s: half_transposes(u, gaT[s]), lag=2)
                elif base == "ga":
                    ep_ga(u, hf, a)
                elif base == "pa":
                    ep_pa(u, hf, a)
                elif base == "gb":
                    ep_gb(u, hf, a)
                elif base == "pb":
                    ep_pb(u, hf, a)
                    if hf == 1:
                        defer(lambda u=u, s=s: half_transposes(u, gbT[s]), lag=2)
                elif base == "wo":
                    ep_wo(u, hf, a)
                    if wo_lagq:
                        wo_load(*wo_lagq.pop(0))
                    if wo_queue:
                        wo_lagq.append(wo_queue.pop(0))
                den_ = max(1, n_units_here - 2)
                tgt = (n_cast * jproc + den_ - 1) // den_
                while n_emitted < min(tgt, n_cast):
                    cast_ops[n_emitted]()
                    n_emitted += 1
                if not second_half_issued and n_emitted >= n_cast // 2:
                    if idx + 2 < NFP:
                        issue_dma_half(idx + 2, 1)
                    second_half_issued = True
                if not scratch_done[0] and n_emitted >= n_cast:
                    scratch_write(idx + 1, wp_next[0], nsrc_)
                    scratch_done[0] = True
            while n_emitted < n_cast:
                cast_ops[n_emitted]()
                n_emitted += 1
            if not scratch_done[0]:
                scratch_write(idx + 1, wp_next[0], nsrc_)
            if not second_half_issued and idx + 2 < NFP:
                issue_dma_half(idx + 2, 1)
            wp_cur = wp_next

        flush(force=True)
        S.emit()
    return nc, S.stats


def _const_tables(core):
    i = np.arange(128)
    j = np.arange(128)[:, None]
    q = np.arange(128)[None, :]
    masks = np.zeros((20, 128, 128), np.float32)
    masks[0] = (j >= q)
    masks[1] = (j <= q)
    mf = (j >= q)
    if core == 0:
        mf = mf & (j >= 128 - N_META)
    masks[2] = mf
    sj, tj = j // 8, j % 8
    sq, tq = q // 8, q % 8
    masks[3] = (sj == sq) & (tj <= tq)
    for s in range(NSEQ):
        masks[4 + s] = (sq == s) & (j >= tq)
    masks = ((masks - 1.0) * 30000.0).astype(np.float32)
    sh = np.zeros((8, 128, 128), np.float32)
    tp_ = np.arange(128)[:, None]
    t = np.arange(128)[None, :]
    sh[0] = (t == tp_ + 1)
    sh[1] = (t == tp_ + 2)
    sh[2] = (tp_ == 127) & (t == 0)
    sh[3] = ((tp_ == 126) & (t == 0)) | ((tp_ == 127) & (t == 1))
    sh[4] = (t == tp_ + 1) & (tp_ // 8 == t // 8)
    sh[5] = (t == tp_ + 2) & (tp_ // 8 == t // 8)
    sh[6] = (tp_ < 32) & (tp_ % 2 == 1) & (t == (tp_ // 2) * 8)
    sh[7] = (tp_ < 32) & (((tp_ % 2 == 0) & (t == (tp_ // 2) * 8)) | ((tp_ % 2 == 1) & (t == (tp_ // 2) * 8 + 1)))
    pos = np.zeros((NU, 128), np.float32)
    base = N_META + core * 2048
    pos[0] = base - 128 + i
    for u in range(1, 17):
        pos[u] = base + (u - 1) * 128 + i
    pos[17] = PAST_LEN + (i % 8)
    inv = np.power(np.float32(10000.0), -np.arange(32, dtype=np.float32) * np.float32(2.0 / 64)).astype(np.float32)
    ang = (pos.reshape(-1, 1).astype(np.float32) * inv[None, :]).astype(np.float32)
    cosr = np.cos(ang.astype(np.float64)).astype(np.float32)
    sinr = np.sin(ang.astype(np.float64)).astype(np.float32)
    return masks, sh, cosr, sinr


_PROGRAM = None


def make_in_maps(x_prompt, x_sample, cache_k, cache_v, state_conv, meta_tokens, norm_w, w_in,
                 q_norm_w, k_norm_w, sinks, conv_w, w_proj_a, w_proj_b, w_out):
    f = lambda a: np.ascontiguousarray(np.asarray(a, dtype=np.float32))
    xp = f(x_prompt)[0]
    xs = f(x_sample)
    ck = f(cache_k)[0].reshape(128, 128, 256)
    cv = f(cache_v)[0].reshape(128, 128, 256)
    sc = f(state_conv)[0]
    meta = f(meta_tokens)
    shared = {"w_in": f(w_in)[0], "w_pa": f(w_proj_a)[0], "w_pb": f(w_proj_b)[0], "w_out": f(w_out)[0],
              "norm_w": f(norm_w)[0], "q_norm_w": f(q_norm_w)[0], "k_norm_w": f(k_norm_w)[0],
              "sinks": f(sinks)[0], "conv_w": f(conv_w)[0], "ident": np.eye(128, dtype=np.float32)}
    in_maps = []
    for c in range(NCORES):
        xin = np.zeros((NU * 128, D), np.float32)
        if c == 0:
            xin[128 - N_META:128] = meta
        else:
            xin[0:128] = xp[c * 2048 - 128:c * 2048]
        xin[128:128 + 2048] = xp[c * 2048:(c + 1) * 2048]
        xin[17 * 128:18 * 128] = xs[c * NSEQ:(c + 1) * NSEQ].reshape(128, D)
        masks, sh, cosr, sinr = _const_tables(c)
        m = dict(shared)
        m.update({"xin": xin, "cache_k": np.ascontiguousarray(ck[c * NSEQ:(c + 1) * NSEQ]),
                  "cache_v": np.ascontiguousarray(cv[c * NSEQ:(c + 1) * NSEQ]),
                  "state_conv": np.ascontiguousarray(sc[c * NSEQ:(c + 1) * NSEQ].reshape(NSEQ * 2, D)),
                  "cosr": cosr, "sinr": sinr, "masks": masks, "shifts": sh})
        in_maps.append(m)
    return in_maps


def kernel(**inputs):
    global _PROGRAM
    in_maps = make_in_maps(**inputs)
    if _PROGRAM is None:
        _PROGRAM = build_program()[0]
    res = run_bass_kernel_spmd(_PROGRAM, in_maps, core_ids=list(range(NCORES)))
    r = res.results
    y_prompt = np.concatenate([r[c]["y"][0:2048] for c in range(NCORES)], 0).reshape(1, 16384, D)
    y_sample = np.concatenate([r[c]["y"][2048:2176].reshape(NSEQ, 8, D) for c in range(NCORES)], 0)
    nk_p = r[NCORES - 1]["nk_p"].reshape(1, 1, 128, 4, 64)
    nv_p = r[NCORES - 1]["nv_p"].reshape(1, 1, 128, 4, 64)
    nc_p = r[NCORES - 1]["nc_p"].reshape(1, 1, 2, D)
    nk_s = np.concatenate([r[c]["nk_s"] for c in range(NCORES)], 0).reshape(1, 128, 128, 4, 64)
    nv_s = np.concatenate([r[c]["nv_s"] for c in range(NCORES)], 0).reshape(1, 128, 128, 4, 64)
    nc_s = np.concatenate([r[c]["nc_s"] for c in range(NCORES)], 0).reshape(1, 128, 2, D)
    return tuple(np.ascontiguousarray(a.astype(np.float32)) for a in
                 (y_prompt, y_sample, nk_p, nv_p, nc_p, nk_s, nv_s, nc_s))
```

```python
import contextlib
import numpy as np
import concourse.bass as bass
import concourse.mybir as mybir
from concourse.bass_utils import run_bass_kernel_spmd

F32 = mybir.dt.float32
BF16 = mybir.dt.bfloat16
AF = mybir.ActivationFunctionType
ALU = mybir.AluOpType
AX = mybir.AxisListType

D = 1024
NCORES = 8
NU = 18
G = 6
NSEQ = 16
EPS = 1e-6
PAST_LEN = 16384
N_META = 16
IN_W = 8704


class Buf:
    __slots__ = ("name", "t", "last_w", "readers", "ld_stream", "st_stream", "excl")

    def __init__(self, name, t=None, excl=False):
        self.name = name
        self.t = t
        self.excl = excl
        self.last_w = None
        self.readers = {}
        self.ld_stream = None
        self.st_stream = None

    def __getitem__(self, k):
        return self.t[k]


class Sched:
    def __init__(self, nc):
        self.nc = nc
        self.ops = []
        self.eng_pos = {}
        self.nstream = 0
        self.dma_streams = set()
        self.engs = {"pe": nc.tensor, "act": nc.scalar, "dve": nc.vector,
                     "pool": nc.gpsimd, "sp": nc.sync}

    def new_stream(self, name):
        s = "d%d_%s" % (self.nstream, name)
        self.nstream += 1
        self.dma_streams.add(s)
        return s

    def _rec(self, eng, stream, fn, reads, writes, kind):
        ex = [b for b in reads if b.excl and b not in writes]
        if ex:
            reads = [b for b in reads if not b.excl]
            writes = list(writes) + ex
        oid = len(self.ops)
        deps = set()
        for b in reads:
            if b.last_w is not None:
                deps.add(b.last_w)
        for b in writes:
            if b.last_w is not None:
                deps.add(b.last_w)
            for r in b.readers.values():
                deps.add(r)
        pos = self.eng_pos.get(eng, 0)
        self.eng_pos[eng] = pos + 1
        self.ops.append({"eng": eng, "stream": stream, "fn": fn, "deps": deps,
                         "kind": kind, "epos": pos})
        for b in reads:
            if b not in writes:
                b.readers[stream] = oid
        for b in writes:
            b.last_w = oid
            b.readers = {}
        return oid

    def op(self, eng, fn, reads=(), writes=()):
        return self._rec(eng, eng, fn, list(reads), list(writes), "eng")

    def dma(self, fn, reads=(), writes=(), queue="sp", stream=None):
        reads = list(reads)
        writes = list(writes)
        if stream is None:
            if writes:
                b = writes[0]
                if b.ld_stream is None:
                    b.ld_stream = self.new_stream("ld_" + b.name)
                stream = b.ld_stream
            else:
                b = reads[0]
                if b.st_stream is None:
                    b.st_stream = self.new_stream("st_" + b.name)
                stream = b.st_stream
        return self._rec(queue, stream, fn, reads, writes, "dma")

    def fence(self, bufs, engines=("pe", "act", "dve", "pool", "sp")):
        for e in engines:
            self._rec(e, e, None, list(bufs), [], "nop")
        for b in bufs:
            b.last_w = None
            b.readers = {}

    def barrier(self, bufs, engines=("pe", "act", "dve", "pool", "sp")):
        for e in engines:
            oid = len(self.ops)
            deps = set()
            for b in bufs:
                if b.last_w is not None:
                    deps.add(b.last_w)
                deps.update(b.readers.values())
            pos = self.eng_pos.get(e, 0)
            self.eng_pos[e] = pos + 1
            self.ops.append({"eng": e, "stream": e, "fn": None, "deps": deps, "kind": "nop", "epos": pos})
        for b in bufs:
            b.last_w = None
            b.readers = {}

    def emit(self):
        nc = self.nc
        ops = self.ops
        n = len(ops)
        spos = [0] * n
        cnt = {}
        for i, o in enumerate(ops):
            c = cnt.get(o["stream"], 0) + 1
            cnt[o["stream"]] = c
            spos[i] = c
        waited = {}
        waits = [None] * n
        signal = [False] * n
        for i, o in enumerate(ops):
            e = o["eng"]
            need = {}
            for d in o["deps"]:
                p = ops[d]
                if p["kind"] == "nop":
                    continue
                ps = p["stream"]
                if p["kind"] != "dma" and p["eng"] == e:
                    if e == "pe" or e == "sp":
                        continue
                    if o["epos"] - p["epos"] > 2:
                        continue
                if waited.get((e, ps), 0) >= spos[d]:
                    continue
                if need.get(ps, (0, None))[0] < spos[d]:
                    need[ps] = (spos[d], d)
            wl = []
            for ps, (sp_, d) in need.items():
                waited[(e, ps)] = sp_
                wl.append(d)
                signal[d] = True
            waits[i] = wl
        for i, o in enumerate(ops):
            if o["kind"] == "dma":
                signal[i] = True
        val = [0] * n
        run = {}
        for i, o in enumerate(ops):
            if signal[i]:
                inc = 16 if o["kind"] == "dma" else 1
                run[o["stream"]] = run.get(o["stream"], 0) + inc
                val[i] = run[o["stream"]]
        streams = sorted(set(o["stream"] for i, o in enumerate(ops) if signal[i]))
        sems = {}
        with contextlib.ExitStack() as st:
            for s in streams:
                sems[s] = st.enter_context(nc.semaphore("s_" + s))
            for i, o in enumerate(ops):
                eng = self.engs[o["eng"]]
                for d in waits[i]:
                    eng.wait_ge(sems[ops[d]["stream"]], val[d])
                if o["fn"] is None:
                    continue
                ins = o["fn"](eng) if o["kind"] == "dma" else o["fn"]()
                if signal[i]:
                    ins.then_inc(sems[o["stream"]], 16 if o["kind"] == "dma" else 1)
            for s in streams:
                if s in self.dma_streams:
                    nc.sync.wait_ge(sems[s], run[s])
        self.stats = {"n_ops": n, "n_waits": sum(len(w) for w in waits),
                      "n_signal": sum(signal), "n_sems": len(streams)}


OFF = {"q": 0, "k": 1024, "v": 1280, "za": 1536, "b": 2560, "c": 3584, "h": 4608,
       "zb": 5632, "ga": 6656, "gb": 7680}

STAGES = [
    ("c0", "win", OFF["c"]), ("h0", "win", OFF["h"]),
    ("c1", "win", OFF["c"] + 512), ("h1", "win", OFF["h"] + 512),
    ("b0", "win", OFF["b"]), ("b1", "win", OFF["b"] + 512),
    ("zb0", "win", OFF["zb"]), ("zb1", "win", OFF["zb"] + 512),
    ("q0", "win", OFF["q"]), ("q1", "win", OFF["q"] + 512),
    ("kv", "win", OFF["k"]),
    ("za0", "win", OFF["za"]), ("za1", "win", OFF["za"] + 512),
    ("ga0", "win", OFF["ga"]), ("pa0", "wpa", 0), ("gb0", "win", OFF["gb"]), ("pb0", "wpb", 0),
    ("ga1", "win", OFF["ga"] + 512), ("pa1", "wpa", 512), ("gb1", "win", OFF["gb"] + 512), ("pb1", "wpb", 512),
    ("wo0", "wout", 0), ("wo1", "wout", 512),
]
HALO_STAGES = ("c0", "h0", "c1", "h1", "kv")


def build_program(limit=None, skip=()):
    nc = bass.Bass("TRN2", target_bir_lowering=False)

    def din(name, shape):
        return nc.dram_tensor(name, shape, F32, kind="ExternalInput").ap()

    def dout(name, shape):
        return nc.dram_tensor(name, shape, F32, kind="ExternalOutput").ap()

    xin = din("xin", [NU * 128, D])
    wsrc = {"win": din("w_in", [D, IN_W]), "wpa": din("w_pa", [D, D]),
            "wpb": din("w_pb", [D, D]), "wout": din("w_out", [D, D])}
    norm_w = din("norm_w", [D])
    qnw = din("q_norm_w", [64])
    knw = din("k_norm_w", [64])
    sinks = din("sinks", [16])
    conv_w = din("conv_w", [3, D])
    cache_k = din("cache_k", [NSEQ, 128, 256])
    cache_v = din("cache_v", [NSEQ, 128, 256])
    state_conv = din("state_conv", [NSEQ * 2, D])
    cosr = din("cosr", [NU * 128, 32])
    sinr = din("sinr", [NU * 128, 32])
    masks_d = din("masks", [20, 128, 128])
    shifts_d = din("shifts", [8, 128, 128])
    ident_d = din("ident", [128, 128])

    NPIECE = len(STAGES)
    wsc = nc.dram_tensor("wsc", [NPIECE, 128, 8 * 512], BF16).ap()

    y_d = dout("y", [17 * 128, D])
    nkp_d = dout("nk_p", [128, 256])
    nvp_d = dout("nv_p", [128, 256])
    ncp_d = dout("nc_p", [2, D])
    nks_d = dout("nk_s", [NSEQ, 128, 256])
    nvs_d = dout("nv_s", [NSEQ, 128, 256])
    ncs_d = dout("nc_s", [NSEQ, 2, D])

    S = Sched(nc)
    with contextlib.ExitStack() as st:
        def sb(name, shape, dt=F32):
            return Buf(name, st.enter_context(nc.sbuf_tensor("s_" + name, shape, dt)))

        def ps(name, shape, dt=F32):
            return Buf(name, st.enter_context(nc.psum_tensor("p_" + name, shape, dt)), excl=True)

        V, A, P, T = nc.vector, nc.scalar, nc.gpsimd, nc.tensor

        acc = [ps("acc%d" % i, [128, 512]) for i in range(2)]
        tp = ps("tp", [128, 8, 128], BF16)
        stb = [ps("st%d" % i, [128, 4, 128]) for i in range(2)]
        ob = [ps("ob%d" % i, [128, 512]) for i in range(3)]
        HB = [(0, 6), (6, 6), (12, 4)]
        tpv = [(tp, tp.t[:])] + [(b_, b_.t[:].bitcast(BF16).rearrange("p a (b c) -> p (a b) c", c=128)) for b_ in stb]
        stq = [(b_, b_.t[:]) for b_ in stb] + [(tp, tp.t[:].bitcast(F32).rearrange("p (a b) c -> p a (b c)", b=2))]
        st_free = [False]
        cpy = [0]

        def tp_pick():
            if st_free[0]:
                return rot(tpv, "tpv")
            return tpv[0]

        def psum_copy(out_ap, in_ap, reads, writes):
            cpy[0] += 1
            if True:
                S.op("act", lambda: A.copy(out=out_ap, in_=in_ap), reads=reads, writes=writes)
            else:
                S.op("dve", lambda: V.tensor_copy(out=out_ap, in_=in_ap), reads=reads, writes=writes)


        def o_head(h):
            bk = 0 if h < 6 else (1 if h < 12 else 2)
            s_ = h - HB[bk][0]
            return ob[bk], ob[bk][:, s_ * 65:(s_ + 1) * 65]

        ident = sb("ident", [128, 128], BF16)
        cs = sb("cs", [128, NU, 32])
        sn = sb("sn", [128, NU, 32])
        masks = sb("masks", [128, 20, 128], BF16)
        shifts = sb("shifts", [128, 8, 128], BF16)
        wbc = [sb("wbc%d" % i, [128, D]) for i in range(3)]
        wq_bc = sb("wq_bc", [128, 64])
        wk_bc = sb("wk_bc", [128, 64])
        esink = sb("esink", [128, 16])
        nw = sb("nw", [128, 8])
        ucarry = sb("ucarry", [128, D], BF16)

        stg = [sb("stg%d" % i, [128, 4, 512]) for i in range(3)]
        wpc = [sb("wpc%d" % i, [128, 8, 512], BF16) for i in range(2)]

        xnT = [sb("xnT%d" % i, [128, 8, 128], BF16) for i in range(G)]
        F32a = [sb("Fa%d" % i, [128, 512]) for i in range(G)]
        F32b = [sb("Fb%d" % i, [128, 512]) for i in range(G)]
        BFa = [sb("Ba%d" % i, [128, 512], BF16) for i in range(G)]
        BFb = [sb("Bb%d" % i, [128, 512], BF16) for i in range(G)]
        gbT = [sb("gbT%d" % i, [128, 8, 128], BF16) for i in range(G)]
        gaT = [sb("gaT%d" % i, [128, 8, 128], BF16) for i in range(G)]
        KT2 = [sb("KT%d" % i, [128, 4, 2, 128], BF16) for i in range(G)]
        V65 = [sb("V65_%d" % i, [128, 4, 65], BF16) for i in range(G)]
        FH = [F32a, F32b]
        BH = [BFa, BFb]

        xbuf = [sb("xbuf%d" % i, [128, D]) for i in range(2)]
        ss1 = [sb("ss1_%d" % i, [128, 1]) for i in range(2)]
        rs1 = [sb("rs1_%d" % i, [128, 1]) for i in range(2)]
        xnb = [sb("xnb%d" % i, [128, D], BF16) for i in range(2)]
        stt = xnb[1]
        tA = [sb("tA%d" % i, [128, 512]) for i in range(2)]
        ssq = [sb("ssq%d" % i, [128, 8]) for i in range(2)]
        rq = [sb("rq%d" % i, [128, 8]) for i in range(2)]
        sqn = sb("sqn", [128, 512])
        qcp = [sb("qcp%d" % i, [128, 512]) for i in range(2)]
        tB, tC = [qcp[0]], [qcp[1]]
        bevb = [sb("bev%d" % i, [128, 512]) for i in range(1)]
        xh = [sb("xh%d" % i, [128, 512]) for i in range(3)]
        qtab = [sb("qtab%d" % i, [128, NU, 32]) for i in range(4)]
        r1 = [sb("r1_%d" % i, [128, 256]) for i in range(1)]
        r2 = [sb("r2_%d" % i, [128, 256]) for i in range(1)]
        kf = [sb("kf%d" % i, [128, 256]) for i in range(1)]
        vf = [sb("vf%d" % i, [128, 256]) for i in range(1)]
        kdup = [sb("kdup%d" % i, [128, 4, 2, 64], BF16) for i in range(G)]
        PT = [sb("PT%d" % i, [128, 4, 128], BF16) for i in range(4)]
        dsum = [sb("dsum%d" % i, [128, 16]) for i in range(2)]
        rden = [sb("rden%d" % i, [128, 16]) for i in range(2)]
        ckb = [Buf("ckb%d" % i, xbuf[0].t[:, i * 256:(i + 1) * 256]) for i in range(4)]
        cvb = [Buf("cvb%d" % i, xh[i // 2].t[:, (i % 2) * 256:(i % 2 + 1) * 256]) for i in range(4)]
        KTc = [Buf("KTc%d" % i, xbuf[1].t[:, i * 512:(i + 1) * 512].bitcast(BF16).rearrange("p (g v k) -> p g v k", g=4, v=2))
               for i in range(2)]
        V65c = [Buf("V65c%d" % i, xnb[0].t[:, i * 512:i * 512 + 260].rearrange("p (g e) -> p g e", e=65)) for i in range(2)]

        ctr = {}
        deferred = []
        mmc = [0]

        def defer(fn, lag=1):
            deferred.append((mmc[0] + lag, fn))

        def flush(force=False):
            progress = True
            while progress:
                progress = False
                for it in list(deferred):
                    if force or it[0] <= mmc[0]:
                        deferred.remove(it)
                        it[1]()
                        progress = True
                        break

        def rot(lst, key):
            i = ctr.get(key, 0)
            ctr[key] = i + 1
            return lst[i % len(lst)]

        cst = S.new_stream("const")

        def cdma(dst, src_ap, dst_ap=None):
            S.dma(lambda e: e.dma_start(out=(dst[:] if dst_ap is None else dst_ap), in_=src_ap),
                  writes=[dst], stream=cst)

        cdma(xbuf[0], ident_d, xbuf[0][:, 0:128])
        cdma(cs, cosr.rearrange("(u p) f -> p u f", p=128))
        cdma(sn, sinr.rearrange("(u p) f -> p u f", p=128))
        cdma(xbuf[1], shifts_d.rearrange("m p f -> p m f"), xbuf[1][:, 0:1024].rearrange("p (m f) -> p m f", f=128))
        for i in range(3):
            cdma(wbc[i], conv_w[i:i + 1, :].broadcast_to([128, D]))
        cdma(wq_bc, qnw.unsqueeze(0).broadcast_to([128, 64]))
        cdma(wk_bc, knw.unsqueeze(0).broadcast_to([128, 64]))
        cdma(esink, sinks.unsqueeze(0).broadcast_to([128, 16]))
        S.dma(lambda e: e.dma_start(out=nw[:], in_=norm_w.rearrange("(k p) -> p k", p=128),
                                    allow_slow_non_contiguous=True), writes=[nw], stream=cst)
        MSPL = [(0, 7), (7, 7), (14, 6)]
        for i, (m0_, mn_) in enumerate(MSPL):
            cdma(stg[i], masks_d[m0_:m0_ + mn_].rearrange("m p f -> p m f"),
                 stg[i][:, :, :].rearrange("p a b -> p (a b)")[:, 0:mn_ * 128].rearrange("p (m f) -> p m f", f=128))
        S.fence([xbuf[0], xbuf[1], cs, sn, wbc[0], wbc[1], wbc[2], wq_bc, wk_bc, esink, nw] + stg)
        S.op("dve", lambda: V.tensor_copy(out=ident[:], in_=xbuf[0][:, 0:128]), reads=[xbuf[0]], writes=[ident])
        S.op("dve", lambda: V.tensor_copy(out=shifts[:].rearrange("p m f -> p (m f)"), in_=xbuf[1][:, 0:1024]),
             reads=[xbuf[1]], writes=[shifts])
        for i, (m0_, mn_) in enumerate(MSPL):
            S.op("pool", lambda i=i, m0_=m0_, mn_=mn_: P.tensor_copy(
                out=masks[:, m0_:m0_ + mn_, :].rearrange("p m f -> p (m f)"),
                in_=stg[i][:, :, :].rearrange("p a b -> p (a b)")[:, 0:mn_ * 128]), reads=[stg[i]], writes=[masks])
        S.op("pool", lambda: P.memset(ucarry[:], 0.0), writes=[ucarry])
        S.op("act", lambda: A.activation(out=esink[:], in_=esink[:], func=AF.Exp), reads=[esink], writes=[esink])
        for ti, (tb_, lo) in enumerate([(cs, 0), (sn, 32), (cs, 32), (sn, 0)]):
            S.op("dve", lambda ti=ti, tb_=tb_, lo=lo: V.tensor_tensor(
                out=qtab[ti][:], in0=tb_[:], in1=wq_bc[:, lo:lo + 32].unsqueeze(1).broadcast_to([128, NU, 32]),
                op=ALU.mult), reads=[tb_, wq_bc], writes=[qtab[ti]])
        for kt_ in KT2:
            S.op("pool", lambda kt_=kt_: P.memset(kt_[:], 0.0), writes=[kt_])
        for i in range(G):
            S.op("pool", lambda i=i: P.memset(V65[i][:, :, 64:65], 1.0), writes=[V65[i]])

        def piece_dma_half(src, col0, half):
            w = wsrc[src]
            sg = rot(stg, "stg")
            S.dma(lambda e: e.dma_start(
                out=sg[:], in_=w[half * 512:(half + 1) * 512, col0:col0 + 512].rearrange("(k p) n -> p k n", p=128)),
                writes=[sg])
            return sg

        def piece_cast(src, hs):
            wp = rot(wpc, "wpc")
            opsl = []
            for half in range(2):
                sg = hs[half]
                if src == "win":
                    for kk in range(4):
                        k = half * 4 + kk
                        if False:
                            opsl.append(lambda sg=sg, kk=kk, k=k: S.op("pool", lambda: P.tensor_scalar(
                                out=wp[:, k, :], in0=sg[:, kk, :], scalar1=nw[:, k:k + 1], scalar2=1.0,
                                op0=ALU.mult, op1=ALU.mult), reads=[sg, nw], writes=[wp]))
                        else:
                            opsl.append(lambda sg=sg, kk=kk, k=k: S.op("act", lambda: A.activation(
                                out=wp[:, k, :], in_=sg[:, kk, :], func=AF.Copy, scale=nw[:, k:k + 1]),
                                reads=[sg, nw], writes=[wp]))
                else:
                    for q2 in range(2):
                        opsl.append(lambda sg=sg, half=half, q2=q2: S.op("dve", lambda: V.tensor_copy(
                            out=wp[:, half * 4 + q2 * 2:half * 4 + q2 * 2 + 2, :], in_=sg[:, q2 * 2:q2 * 2 + 2, :]),
                            reads=[sg], writes=[wp]))
            return wp, opsl

        def kind(u):
            return "halo" if u == 0 else ("sample" if u == 17 else "prompt")

        def transposes(src_fn, dstT, srcbufs, n=8, dst_sl=None):
            tb, tvw = tp_pick()
            for k in range(n):
                S.op("pe", lambda k=k: T.transpose(out=tvw[:, k, :], in_=src_fn(k), identity=ident[:]),
                     reads=srcbufs + [ident], writes=[tb])
            psum_copy(dstT[:] if dst_sl is None else dst_sl, tvw[:, 0:n, :], [tb], [dstT])

        def stage_x_load(u):
            xb = rot(xbuf, "xbuf")
            S.dma(lambda e: e.dma_start(out=xb[:], in_=xin[u * 128:(u + 1) * 128, :]), writes=[xb])
            return xb

        def stage_x_p1(u, xb):
            ss = rot(ss1, "ss1")
            rs = rot(rs1, "rs1")
            xn_ = rot(xnb, "xnb")
            S.op("act", lambda: A.activation(out=xn_[:], in_=xb[:], func=AF.Square, accum_out=ss[:]),
                 reads=[xb], writes=[xn_, ss])
            S.op("act", lambda: A.activation(out=rs[:], in_=ss[:], func=AF.Sqrt, bias=EPS, scale=1.0 / D),
                 reads=[ss], writes=[rs])
            return rs, xn_

        def stage_x_p2(u, xb, rs, xn_):
            S.op("dve", lambda: V.reciprocal(out=rs[:], in_=rs[:]), reads=[rs], writes=[rs])
            S.op("act", lambda: A.activation(out=xn_[:], in_=xb[:], func=AF.Copy, scale=rs[:, 0:1]),
                 reads=[xb, rs], writes=[xn_])

        def stage_x_front(u, xb):
            rs, xn_ = stage_x_p1(u, xb)
            stage_x_p2(u, xb, rs, xn_)
            return xn_

        def xpre_trigger(un):
            xb = stage_x_load(un)

            def p1():
                rs, xn_ = stage_x_p1(un, xb)

                def p2():
                    stage_x_p2(un, xb, rs, xn_)
                    defer(lambda: stage_x_back(un, xn_), lag=2)
                defer(p2, lag=1)
            defer(p1, lag=2)

        def stage_x_back(u, xn_):
            transposes(lambda k: xn_[:, k * 128:(k + 1) * 128], xnT[u % G], [xn_])

        def stage_x(u):
            stage_x_back(u, stage_x_front(u, stage_x_load(u)))

        acc3 = acc + [ob[2]]
        acc_wide = [False]

        def main_mm(u, lhsT, wpair):
            wpb, wpv = wpair
            a = rot(acc3, "acc3") if acc_wide[0] else rot(acc, "acc")
            for k in range(8):
                S.op("pe", lambda k=k: T.matmul(a[:], lhsT=lhsT[:, k, :], rhs=wpv[:, k, :], start=(k == 0), stop=(k == 7)),
                     reads=[lhsT, wpb], writes=[a])
            return a

        def ep_c(u, hf, a):
            f = FH[hf][u % G]
            S.op("act", lambda: A.copy(out=f[:], in_=a[:]), reads=[a], writes=[f])

        def ep_h(u, hf, a):
            s = u % G
            f = FH[hf][s]
            bh = BH[hf][s]
            S.op("dve", lambda: V.tensor_tensor(out=f[:], in0=f[:], in1=a[:], op=ALU.mult), reads=[f, a], writes=[f])
            S.op("act", lambda: A.copy(out=bh[:], in_=f[:]), reads=[f], writes=[bh])
            if u == 16:
                S.dma(lambda e: e.dma_start(out=ncp_d[:, hf * 512:(hf + 1) * 512], in_=f[126:128, :]), reads=[f])
            if u == 17:
                for sq_ in range(NSEQ):
                    S.dma(lambda e, sq_=sq_: e.dma_start(out=ncs_d[sq_, :, hf * 512:(hf + 1) * 512],
                                                         in_=f[sq_ * 8 + 6:sq_ * 8 + 8, :]), reads=[f])

        shsel = {}

        def flat(b_):
            return b_[:].rearrange("p a b -> p (a b)") if len(b_[:].shape) == 3 else b_[:]

        def pre_b(u, hf):
            s = u % G
            ub = BH[hf][s]
            smp = (u == 17)
            m0 = 4 if smp else 0
            if smp:
                prev, prev_ap = stt, stt[:, hf * 512:(hf + 1) * 512]
            elif s == 0:
                prev, prev_ap = ucarry, ucarry[:, hf * 512:(hf + 1) * 512]
            else:
                prev, prev_ap = BH[hf][s - 1], BH[hf][s - 1][:]
            par = ctr.get("shpar", 0)
            ctr["shpar"] = par + 1
            sx, sy = (stb[0], stb[1]) if par % 2 == 0 else (ob[0], ob[1])
            shsel[(u, hf)] = (sx, sy)
            S.op("pe", lambda: T.matmul(flat(sx), lhsT=shifts[:, m0 + 0, :], rhs=ub[:],
                                        start=True, stop=False), reads=[shifts, ub], writes=[sx])
            S.op("pe", lambda: T.matmul(flat(sx), lhsT=shifts[:, m0 + 2, :], rhs=prev_ap,
                                        start=False, stop=True), reads=[shifts, prev], writes=[sx])
            S.op("pe", lambda: T.matmul(flat(sy), lhsT=shifts[:, m0 + 1, :], rhs=ub[:],
                                        start=True, stop=False), reads=[shifts, ub], writes=[sy])
            S.op("pe", lambda: T.matmul(flat(sy), lhsT=shifts[:, m0 + 3, :], rhs=prev_ap,
                                        start=False, stop=True), reads=[shifts, prev], writes=[sy])

        def ep_b(u, hf, a):
            s = u % G
            f = FH[hf][s]
            sx, sy = shsel.pop((u, hf))
            SX, SY = [sx], [sy]
            bev = rot(bevb, "bev")
            S.op("act", lambda: A.copy(out=bev[:], in_=a[:]), reads=[a], writes=[bev])
            t1 = rot(tA, "tA")
            t2 = rot(tB, "tB")
            t3 = rot(tC, "tC")
            cs_ = slice(hf * 512, (hf + 1) * 512)
            S.op("dve", lambda: V.tensor_tensor(out=t1[:], in0=flat(sx), in1=wbc[1][:, cs_],
                                                op=ALU.mult), reads=SX + [wbc[1]], writes=[t1])
            S.op("dve", lambda: V.tensor_tensor(out=t2[:], in0=flat(sy), in1=wbc[0][:, cs_],
                                                op=ALU.mult), reads=SY + [wbc[0]], writes=[t2])
            S.op("pool", lambda: P.tensor_tensor(out=t3[:], in0=f[:], in1=wbc[2][:, cs_], op=ALU.mult),
                 reads=[f, wbc[2]], writes=[t3])
            S.op("dve", lambda: V.tensor_tensor(out=t1[:], in0=t1[:], in1=t2[:], op=ALU.add), reads=[t1, t2], writes=[t1])
            S.op("pool", lambda: P.tensor_tensor(out=t1[:], in0=t1[:], in1=t3[:], op=ALU.add), reads=[t1, t3], writes=[t1])
            S.op("dve", lambda: V.tensor_tensor(out=f[:], in0=t1[:], in1=bev[:], op=ALU.mult), reads=[t1, bev], writes=[f])

        def ep_zb(u, hf, a):
            s = u % G
            f = FH[hf][s]
            bh = BH[hf][s]
            t1 = rot(tA, "tA")
            S.op("act", lambda: A.activation(out=t1[:], in_=a[:], func=AF.Silu), reads=[a], writes=[t1])
            S.op("pool", lambda: P.tensor_tensor(out=bh[:], in0=f[:], in1=t1[:], op=ALU.mult), reads=[f, t1], writes=[bh])

        def half_transposes(u, dstT):
            s = u % G
            tb, tvw = tp_pick()
            for hf in range(2):
                bh = BH[hf][s]
                for kk in range(4):
                    k = hf * 4 + kk
                    S.op("pe", lambda k=k, kk=kk, bh=bh: T.transpose(out=tvw[:, k, :], in_=bh[:, kk * 128:(kk + 1) * 128],
                                                                     identity=ident[:]), reads=[bh, ident], writes=[tb])
            psum_copy(dstT[:], tvw[:, :, :], [tb], [dstT])

        def norm_rope(u, a_ap, a_buf, nh, w_bc, out_ap_fn, out_bufs):
            sq_ = sqn
            ss = rot(ssq, "ssq")
            r = rot(rq, "rq")
            q_ = rot(qcp, "qcp")
            w = nh * 64
            S.op("act", lambda: A.activation(out=sq_[:, 0:w], in_=a_ap, func=AF.Square), reads=[a_buf], writes=[sq_])
            S.op("act", lambda: A.copy(out=q_[:, 0:w], in_=a_ap), reads=[a_buf], writes=[q_])
            S.op("dve", lambda: V.tensor_reduce(out=ss[:, 0:nh], in_=sq_[:, 0:w].rearrange("p (h d) -> p h d", d=64),
                                                axis=AX.X, op=ALU.add), reads=[sq_], writes=[ss])
            S.op("act", lambda: A.activation(out=r[:, 0:nh], in_=ss[:, 0:nh], func=AF.Sqrt, bias=EPS, scale=1.0 / 64),
                 reads=[ss], writes=[r])
            S.op("dve", lambda: V.reciprocal(out=r[:, 0:nh], in_=r[:, 0:nh]), reads=[r], writes=[r])
            S.op("dve", lambda: V.tensor_tensor(
                out=q_[:, 0:w].rearrange("p (h d) -> p h d", d=64), in0=q_[:, 0:w].rearrange("p (h d) -> p h d", d=64),
                in1=r[:, 0:nh].unsqueeze(2).broadcast_to([128, nh, 64]), op=ALU.mult), reads=[q_, r], writes=[q_])
            if w_bc is not None:
                S.op("pool", lambda: P.tensor_tensor(
                    out=q_[:, 0:w].rearrange("p (h d) -> p h d", d=64), in0=q_[:, 0:w].rearrange("p (h d) -> p h d", d=64),
                    in1=w_bc[:, :].unsqueeze(1).broadcast_to([128, nh, 64]), op=ALU.mult), reads=[q_, w_bc], writes=[q_])
            qv = q_[:, 0:w].rearrange("p (h t d) -> p h t d", t=2, d=32)
            x1 = qv[:, :, 0, :]
            x2 = qv[:, :, 1, :]
            if w_bc is not None:
                tbs = [cs, sn, cs, sn]
            else:
                tbs = qtab
            tv = [t_[:, u, :].unsqueeze(1).broadcast_to([128, nh, 32]) for t_ in tbs]
            cb, sb_, cb2, sb2 = tv[0], tv[1], tv[2], tv[3]
            cs_b, sn_b, cs_b2, sn_b2 = tbs
            a1, a2 = r1[0], r2[0]
            a3, a4 = a1, a2

            def v3(t):
                return t[:, 0:nh * 32].rearrange("p (h d) -> p h d", d=32)
            S.op("dve", lambda: V.tensor_tensor(out=v3(a1), in0=x1, in1=cb, op=ALU.mult), reads=[q_, cs_b], writes=[a1])
            S.op("pool", lambda: P.tensor_tensor(out=v3(a2), in0=x2, in1=sb_, op=ALU.mult), reads=[q_, sn_b], writes=[a2])
            S.op("dve", lambda: V.tensor_tensor(out=out_ap_fn(0), in0=v3(a1), in1=v3(a2), op=ALU.subtract),
                 reads=[a1, a2], writes=out_bufs)
            S.op("dve", lambda: V.tensor_tensor(out=v3(a3), in0=x2, in1=cb2, op=ALU.mult), reads=[q_, cs_b2], writes=[a3])
            S.op("pool", lambda: P.tensor_tensor(out=v3(a4), in0=x1, in1=sb2, op=ALU.mult), reads=[q_, sn_b2], writes=[a4])
            S.op("pool", lambda: P.tensor_tensor(out=out_ap_fn(1), in0=v3(a3), in1=v3(a4), op=ALU.add),
                 reads=[a3, a4], writes=out_bufs)

        def ep_q(u, hf, a):
            s = u % G
            bh = BH[hf][s]
            bv = bh[:].rearrange("p (h t d) -> p h t d", t=2, d=32)
            norm_rope(u, a[:], a, 8, None, lambda part: bv[:, :, part, :], [bh])

        def q_transposes(u):
            s = u % G
            half_transposes(u, gaT[s])

        def kt_transposes(kd, ktdst):
            for g in range(4):
                S.op("pe", lambda g=g: T.transpose(out=tp[:, g, :], in_=kd[:, g, :, :].rearrange("p a d -> p (a d)"),
                                                   identity=ident[:]), reads=[kd, ident], writes=[tp])
            S.op("dve", lambda: V.tensor_copy(out=ktdst[0:64, :, 0, :], in_=tp[0:64, 0:4, :]), reads=[tp], writes=[ktdst])
            S.op("dve", lambda: V.tensor_copy(out=ktdst[64:128, :, 1, :], in_=tp[64:128, 0:4, :]), reads=[tp], writes=[ktdst])

        def cast_kv(kf_, vf_, kd, v65dst):
            S.op("act", lambda: A.copy(out=kd[:], in_=kf_[:].rearrange("p (g d) -> p g d", d=64).unsqueeze(2).broadcast_to([128, 4, 2, 64])),
                 reads=[kf_], writes=[kd])
            S.op("dve", lambda: V.tensor_copy(out=v65dst[:, :, 0:64], in_=vf_[:].rearrange("p (g d) -> p g d", d=64)),
                 reads=[vf_], writes=[v65dst])

        def ep_kv(u, a):
            s = u % G
            kd = kdup[s]
            if u in (16, 17):
                kf_, vf_ = kf[0], vf[0]
                kv3 = kf_[:].rearrange("p (h t d) -> p h t d", t=2, d=32)
                norm_rope(u, a[:, 0:256], a, 4, wk_bc, lambda part: kv3[:, :, part, :], [kf_])
                S.op("act", lambda: A.copy(out=vf_[:], in_=a[:, 256:512]), reads=[a], writes=[vf_])
                cast_kv(kf_, vf_, kd, V65[s])
                if u == 16:
                    S.dma(lambda e: e.dma_start(out=nkp_d, in_=kf_[:]), reads=[kf_])
                    S.dma(lambda e: e.dma_start(out=nvp_d, in_=vf_[:]), reads=[vf_])
                else:
                    for sq_ in range(NSEQ):
                        S.dma(lambda e, sq_=sq_: e.dma_start(out=nks_d[sq_, 120:128, :], in_=kf_[sq_ * 8:(sq_ + 1) * 8, :]),
                              reads=[kf_])
                        S.dma(lambda e, sq_=sq_: e.dma_start(out=nvs_d[sq_, 120:128, :], in_=vf_[sq_ * 8:(sq_ + 1) * 8, :]),
                              reads=[vf_])
            else:
                kd0 = kd[:, :, 0, :].rearrange("p g (t d) -> p g t d", t=2)
                S.op("act", lambda: A.copy(out=V65[s][:, :, 0:64], in_=a[:, 256:512].rearrange("p (g d) -> p g d", d=64)),
                     reads=[a], writes=[V65[s]])
                norm_rope(u, a[:, 0:256], a, 4, wk_bc, lambda part: kd0[:, :, part, :], [kd])
                S.op("pool", lambda: P.tensor_copy(out=kd[:, :, 1, :], in_=kd[:, :, 0, :]), reads=[kd], writes=[kd])

        def attention(u, keyblocks, loaders=None):
            s = u % G
            QT = gaT[s]
            nkb = len(keyblocks)
            items = [(kb, g) for kb in range(nkb) for g in range(4)]
            pts = {}
            kbd = {}

            def need(kb):
                if kb < nkb and kb not in kbd:
                    kbd[kb] = keyblocks[kb]()

            def scores(i):
                kb, g = items[i]
                if loaders is not None and g == 0:
                    for kb2 in range(kb, min(kb + 4, len(loaders))):
                        loaders[kb2]()
                need(kb)
                if g == 3:
                    need(kb + 1)
                KT, _, mi = kbd[kb]
                stt_, stv = rot(stq, "stq")
                pt = rot(PT, "PT")
                pts[i] = pt
                S.op("pe", lambda: T.matmul(stv, lhsT=ident[:],
                                            rhs=masks[:, mi, :].unsqueeze(1).broadcast_to([128, 4, 128]),
                                            start=True, stop=False), reads=[ident, masks], writes=[stt_])
                for var in range(2):
                    S.op("pe", lambda var=var: T.matmul(
                        stv[:, 2 * var:2 * var + 2, :], lhsT=KT[:, g, var, :],
                        rhs=QT[:, 2 * g:2 * g + 2, :], start=False, stop=(var == 1)),
                        reads=[KT, QT], writes=[stt_])
                if "exp" in skip:
                    return
                S.op("act", lambda: A.activation(out=pt[:].rearrange("p a b -> p (a b)"),
                                                 in_=stv.rearrange("p a b -> p (a b)"), func=AF.Exp, scale=0.125),
                     reads=[stt_], writes=[pt])
                pts[i] = pt

            def bank_of(h):
                return 0 if h < 6 else (1 if h < 12 else 2)

            def head_of(g, j):
                return 4 * g + 2 * (j % 2) + (j // 2)
            first_w, last_w = {}, {}
            for i_, (kb_, g_) in enumerate(items):
                for j_ in range(4):
                    bk_ = bank_of(head_of(g_, j_))
                    first_w.setdefault(bk_, (i_, j_))
                    last_w[bk_] = (i_, j_)

            def pv(i):
                kb, g = items[i]
                _, VV, _ = kbd[kb]
                pt = pts.pop(i)
                for j in range(4):
                    if "pv" in skip:
                        break
                    h = head_of(g, j)
                    bk = bank_of(h)
                    obuf, oap = o_head(h)
                    S.op("pe", lambda j=j, oap=oap, bk=bk: T.matmul(oap, lhsT=pt[:, j, :], rhs=VV[:, g, :],
                                                                    start=(first_w[bk] == (i, j)),
                                                                    stop=(last_w[bk] == (i, j))),
                         reads=[pt, VV], writes=[obuf])

            n = len(items)
            LAG = 3
            for i in range(n + LAG):
                if i < n:
                    scores(i)
                if i - LAG >= 0:
                    pv(i - LAG)
            if "onorm" in skip:
                return
            ds = rot(dsum, "dsum")
            rd = rot(rden, "rden")
            for bk, (h0, nh) in enumerate(HB):
                S.op("dve", lambda bk=bk, h0=h0, nh=nh: V.tensor_tensor(
                    out=ds[:, h0:h0 + nh], in0=ob[bk][:, 0:nh * 65].rearrange("p (h e) -> p h e", e=65)[:, :, 64],
                    in1=esink[:, h0:h0 + nh], op=ALU.add), reads=[ob[bk], esink], writes=[ds])
            S.op("dve", lambda: V.reciprocal(out=rd[:], in_=ds[:]), reads=[ds], writes=[rd])
            segs = [(0, 0, 6, F32a[s], 0), (1, 0, 2, F32a[s], 384), (1, 2, 4, F32b[s], 0), (2, 0, 4, F32b[s], 256)]
            for (bk, sl0, nh, dst, c0) in segs:
                h0 = HB[bk][0] + sl0
                S.op("dve", lambda bk=bk, sl0=sl0, nh=nh, dst=dst, c0=c0, h0=h0: V.tensor_tensor(
                    out=dst[:, c0:c0 + nh * 64].rearrange("p (h d) -> p h d", d=64),
                    in0=ob[bk][:, sl0 * 65:(sl0 + nh) * 65].rearrange("p (h e) -> p h e", e=65)[:, :, 0:64],
                    in1=rd[:, h0:h0 + nh].unsqueeze(2).broadcast_to([128, nh, 64]), op=ALU.mult),
                    reads=[ob[bk], rd], writes=[dst])

        cache_ld = {}
        cache_issued = set()

        def cache_loader(sq_):
            def ld():
                if sq_ in cache_issued:
                    return
                cache_issued.add(sq_)
                ck = rot(ckb, "ckb")
                cv = rot(cvb, "cvb")
                S.dma(lambda e: e.dma_start(out=ck[:], in_=cache_k[sq_]), writes=[ck])
                S.dma(lambda e: e.dma_start(out=cv[:], in_=cache_v[sq_]), writes=[cv])
                cache_ld[sq_] = (ck, cv)
            return ld

        def cache_provider(sq_):
            def prov():
                cache_loader(sq_)()
                ck, cv = cache_ld.pop(sq_)
                kt = rot(KTc, "KTc")
                vv = rot(V65c, "V65c")
                kd = rot(kdup[0:2], "kdc")
                cast_kv(ck, cv, kd, vv)
                kt_transposes(kd, kt)
                return (kt, vv, 4 + sq_)
            return prov

        def ep_za(u, hf, a):
            s = u % G
            f = FH[hf][s]
            bh = BH[hf][s]
            t1 = rot(tA, "tA")
            S.op("act", lambda: A.activation(out=t1[:], in_=a[:], func=AF.Silu), reads=[a], writes=[t1])
            S.op("pool", lambda: P.tensor_tensor(out=bh[:], in0=f[:], in1=t1[:], op=ALU.mult), reads=[f, t1], writes=[bh])

        def ep_ga(u, hf, a):
            f = F32a[u % G]
            S.op("act", lambda: A.activation(out=f[:], in_=a[:], func=AF.Sigmoid), reads=[a], writes=[f])

        def ep_pa(u, hf, a):
            f = F32a[u % G]
            S.op("dve", lambda: V.tensor_tensor(out=f[:], in0=f[:], in1=a[:], op=ALU.mult), reads=[f, a], writes=[f])

        def ep_gb(u, hf, a):
            f = F32b[u % G]
            S.op("act", lambda: A.activation(out=f[:], in_=a[:], func=AF.Sigmoid), reads=[a], writes=[f])

        def ep_pb(u, hf, a):
            s = u % G
            t1 = rot(tA, "tA")
            bh = BH[hf][s]
            S.op("dve", lambda: V.tensor_tensor(out=t1[:], in0=F32b[s][:], in1=a[:], op=ALU.mult),
                 reads=[F32b[s], a], writes=[t1])
            S.op("pool", lambda: P.tensor_tensor(out=bh[:], in0=t1[:], in1=F32a[s][:], op=ALU.add),
                 reads=[t1, F32a[s]], writes=[bh])

        wo_buf = {}
        wo_queue = []
        wo_lagq = []

        def wo_load(u, hf):
            xb = rot(xh, "xh")
            wo_buf[(u, hf)] = xb
            S.dma(lambda e: e.dma_start(out=xb[:], in_=xin[u * 128:(u + 1) * 128, hf * 512:(hf + 1) * 512]),
                  writes=[xb])

        def ep_wo(u, hf, a):
            xb = wo_buf.pop((u, hf))
            S.op("dve", lambda: V.tensor_tensor(out=xb[:], in0=a[:], in1=xb[:], op=ALU.add), reads=[a, xb], writes=[xb])
            S.dma(lambda e: e.dma_start(out=y_d[(u - 1) * 128:u * 128, hf * 512:(hf + 1) * 512], in_=xb[:]),
                  reads=[xb])

        dd = S.new_stream("d2d")
        if "d2d" not in skip:
            S.dma(lambda e: e.dma_start(out=nks_d[:, 0:120, :], in_=cache_k[:, 8:128, :]), stream=dd)
            S.dma(lambda e: e.dma_start(out=nvs_d[:, 0:120, :], in_=cache_v[:, 8:128, :]), stream=dd)

        xpre_buf = {}
        XPRE = {"pb1": 0, "wo0": 2, "wo1": 4}
        allst = [(p, si) for p in range(3) for si in range(len(STAGES))]
        pend = {}

        def issue_dma_half(idx, half):
            if idx < len(allst):
                _, si = allst[idx]
                pend.setdefault(idx, []).append(piece_dma_half(STAGES[si][1], STAGES[si][2], half))

        def start_cast(idx):
            if idx >= len(allst):
                return None, []
            _, si = allst[idx]
            return piece_cast(STAGES[si][1], pend.pop(idx))

        NFP = NPIECE + 2
        wsc_all = Buf("wsc_all")
        wsc_stream = S.new_stream("wsc")

        def scratch_write(n, wpb, src):
            if n >= NPIECE:
                return
            q_ = "act"
            S.dma(lambda e: e.dma_start(out=wsc[n], in_=wpb[:].rearrange("p k n -> p (k n)")),
                  reads=[wpb], writes=[wsc_all], queue=q_, stream=wsc_stream)

        def bf_view(b_):
            if b_ in wpc:
                return b_.t[:]
            return b_.t[:].bitcast(BF16).rearrange("p a (b c) -> p (a b) c", c=512)

        ring = [stg[(2 * NFP - 1) % 3], stg[(2 * NFP - 2) % 3 - 0], None, None, None]
        h0_, h1_ = (2 * (NFP - 1)) % 3, (2 * (NFP - 1) + 1) % 3
        free_ = [i for i in range(3) if i not in (h0_, h1_)][0]
        ring = [stg[free_], stg[h0_], stg[h1_], wpc[(NFP - 2) % 2], wpc[(NFP - 1) % 2]]
        slot_free_at = {0: NFP - 2, 1: NFP - 1, 2: NFP - 1, 3: NFP - 1, 4: NFP}
        ring_piece = {}
        next_load = [NFP]

        def ring_loads(idx):
            while next_load[0] < len(allst) and next_load[0] <= idx + 3:
                n = next_load[0]
                k_ = (n - NFP) % 5
                if slot_free_at[k_] > idx:
                    break
                b_ = ring[k_]
                v_ = bf_view(b_)
                S.dma(lambda e, v_=v_, n=n: e.dma_start(out=v_.rearrange("p k n -> p (k n)"), in_=wsc[n % NPIECE]),
                      reads=[wsc_all], writes=[b_])
                ring_piece[n] = (b_, v_)
                slot_free_at[k_] = n + 1
                next_load[0] += 1

        issue_dma_half(0, 0)
        issue_dma_half(0, 1)
        wp0, ops0 = start_cast(0)
        wp_cur = (wp0, wp0.t[:])
        for f_ in ops0:
            f_()
        scratch_write(0, wp0, STAGES[0][1])
        issue_dma_half(1, 0)
        issue_dma_half(1, 1)
        for idx, (p, si) in enumerate(allst):
            if limit is not None and idx >= limit:
                break
            name = STAGES[si][0]
            units = list(range(p * G, (p + 1) * G))
            if si == 0 and p == 0:
                for u in units:
                    stage_x(u)
            wp = wp_cur
            ring_loads(idx)
            if idx + 2 < NFP:
                issue_dma_half(idx + 2, 0)
            if idx + 1 < NFP:
                wpn_, cast_ops = start_cast(idx + 1)
                wp_next = (wpn_, wpn_.t[:]) if wpn_ is not None else None
            else:
                cast_ops = []
                wp_next = ring_piece.pop(idx + 1, None)
            nsrc_ = STAGES[allst[idx + 1][1]][1] if idx + 1 < len(allst) else None
            scratch_done = [idx + 1 >= NFP or not cast_ops]
            n_cast = len(cast_ops)
            n_emitted = 0
            second_half_issued = False
            n_units_here = sum(1 for u_ in units if not (kind(u_) == "halo" and name not in HALO_STAGES))
            base = name.rstrip("01")
            hf = 1 if name.endswith("1") else 0
            if p == 2 and name == "h1":
                for hh in range(2):
                    S.dma(lambda e, hh=hh: e.dma_start(out=tA[hh][0:32, :], in_=state_conv[:, hh * 512:(hh + 1) * 512]),
                          writes=[tA[hh]])
                S.op("pool", lambda: P.memset(stt[:], 0.0), writes=[stt])
                for hh in range(2):
                    S.op("pool", lambda hh=hh: P.tensor_copy(out=stt[0:32, hh * 512:(hh + 1) * 512], in_=tA[hh][0:32, :]),
                         reads=[tA[hh]], writes=[stt])
            st_free[0] = name not in ("b0", "b1", "kv", "za0")
            acc_wide[0] = name not in ("kv", "za0", "za1")
            jproc = 0
            proc_units = [u_ for u_ in units if not (kind(u_) == "halo" and name not in HALO_STAGES)]
            if name == "pb1":
                wo_queue.clear()
                wo_queue.extend([(u_, 0) for u_ in proc_units] + [(u_, 1) for u_ in proc_units])
                for _ in range(len(xh)):
                    wo_load(*wo_queue.pop(0))
            for u in units:
                s = u % G
                kd = kind(u)
                if kd == "halo" and name not in HALO_STAGES:
                    continue
                if base in ("pa",):
                    lhs = gaT[s]
                elif base in ("pb",):
                    lhs = gbT[s]
                elif base == "wo":
                    lhs = gbT[s]
                else:
                    lhs = xnT[s]
                if base == "b":
                    pre_b(u, hf)
                a = main_mm(u, lhs, wp)
                mmc[0] += 1
                flush()
                jproc += 1
                if p < 2 and name in XPRE and jproc in (1, 3):
                    xpre_trigger((p + 1) * G + XPRE[name] + (1 if jproc == 3 else 0))
                if base == "c":
                    ep_c(u, hf, a)
                elif base == "h":
                    ep_h(u, hf, a)
                elif base == "b":
                    ep_b(u, hf, a)
                    if hf == 1 and s == G - 1 and u != 17:
                        S.op("pool", lambda s=s: P.tensor_copy(out=ucarry[:, 0:512], in_=BFa[s][:]),
                             reads=[BFa[s]], writes=[ucarry])
                        S.op("pool", lambda s=s: P.tensor_copy(out=ucarry[:, 512:1024], in_=BFb[s][:]),
                             reads=[BFb[s]], writes=[ucarry])
                elif base == "zb":
                    ep_zb(u, hf, a)
                    if hf == 1:
                        defer(lambda u=u, s=s: half_transposes(u, gbT[s]), lag=2)
                elif base == "q":
                    ep_q(u, hf, a)
                    if hf == 1:
                        defer(lambda u=u: q_transposes(u), lag=3)
                elif name == "kv":
                    ep_kv(u, a)

                    jk = proc_units.index(u)
                    nxt_u = proc_units[jk + 1] if jk + 1 < len(proc_units) else None

                    def kv_aux(u=u, s=s, kd=kd, jk=jk, nxt_u=nxt_u):
                        if jk == 0:
                            kt_transposes(kdup[s], KT2[s])
                        if "attn" in skip:
                            pass
                        elif kd == "prompt":
                            ps_ = (u - 1) % G
                            mprev = 2 if u == 1 else 0
                            attention(u, [lambda: (KT2[ps_], V65[ps_], mprev),
                                          lambda: (KT2[s], V65[s], 1)])
                        elif kd == "sample":
                            S.barrier([xbuf[0], xbuf[1], xnb[0], xh[0], xh[1]])
                            for kt_ in KTc:
                                S.op("pool", lambda kt_=kt_: P.memset(kt_[:], 0.0), writes=[kt_])
                            for vv_ in V65c:
                                S.op("pool", lambda vv_=vv_: P.memset(vv_[:, :, 64:65], 1.0), writes=[vv_])
                            kbs = [cache_provider(sq_) for sq_ in range(NSEQ)]
                            kbs.append(lambda: (KT2[s], V65[s], 3))
                            attention(u, kbs, loaders=[cache_loader(sq_) for sq_ in range(NSEQ)])
                            S.barrier(ckb + cvb + KTc + V65c)
                        if nxt_u is not None:
                            kt_transposes(kdup[nxt_u % G], KT2[nxt_u % G])
                    defer(kv_aux, lag=4)
                elif base == "za":
                    ep_za(u, hf, a)
                    if hf == 1:
                        defer(lambda u=u, s=s: half_transposes(u, gaT[s]), lag=2)
                elif base == "ga":
                    ep_ga(u, hf, a)
                elif base == "pa":
                    ep_pa(u, hf, a)
                elif base == "gb":
                    ep_gb(u, hf, a)
                elif base == "pb":
                    ep_pb(u, hf, a)
                    if hf == 1:
                        defer(lambda u=u, s=s: half_transposes(u, gbT[s]), lag=2)
                elif base == "wo":
                    ep_wo(u, hf, a)
                    if wo_lagq:
                        wo_load(*wo_lagq.pop(0))
                    if wo_queue:
                        wo_lagq.append(wo_queue.pop(0))
                den_ = max(1, n_units_here - 2)
                tgt = (n_cast * jproc + den_ - 1) // den_
                while n_emitted < min(tgt, n_cast):
                    cast_ops[n_emitted]()
                    n_emitted += 1
                if not second_half_issued and n_emitted >= n_cast // 2:
                    if idx + 2 < NFP:
                        issue_dma_half(idx + 2, 1)
                    second_half_issued = True
                if not scratch_done[0] and n_emitted >= n_cast:
                    scratch_write(idx + 1, wp_next[0], nsrc_)
                    scratch_done[0] = True
            while n_emitted < n_cast:
                cast_ops[n_emitted]()
                n_emitted += 1
            if not scratch_done[0]:
                scratch_write(idx + 1, wp_next[0], nsrc_)
            if not second_half_issued and idx + 2 < NFP:
                issue_dma_half(idx + 2, 1)
            wp_cur = wp_next

        flush(force=True)
        S.emit()
    return nc, S.stats


def _const_tables(core):
    i = np.arange(128)
    j = np.arange(128)[:, None]
    q = np.arange(128)[None, :]
    masks = np.zeros((20, 128, 128), np.float32)
    masks[0] = (j >= q)
    masks[1] = (j <= q)
    mf = (j >= q)
    if core == 0:
        mf = mf & (j >= 128 - N_META)
    masks[2] = mf
    sj, tj = j // 8, j % 8
    sq, tq = q // 8, q % 8
    masks[3] = (sj == sq) & (tj <= tq)
    for s in range(NSEQ):
        masks[4 + s] = (sq == s) & (j >= tq)
    masks = ((masks - 1.0) * 30000.0).astype(np.float32)
    sh = np.zeros((8, 128, 128), np.float32)
    tp_ = np.arange(128)[:, None]
    t = np.arange(128)[None, :]
    sh[0] = (t == tp_ + 1)
    sh[1] = (t == tp_ + 2)
    sh[2] = (tp_ == 127) & (t == 0)
    sh[3] = ((tp_ == 126) & (t == 0)) | ((tp_ == 127) & (t == 1))
    sh[4] = (t == tp_ + 1) & (tp_ // 8 == t // 8)
    sh[5] = (t == tp_ + 2) & (tp_ // 8 == t // 8)
    sh[6] = (tp_ < 32) & (tp_ % 2 == 1) & (t == (tp_ // 2) * 8)
    sh[7] = (tp_ < 32) & (((tp_ % 2 == 0) & (t == (tp_ // 2) * 8)) | ((tp_ % 2 == 1) & (t == (tp_ // 2) * 8 + 1)))
    pos = np.zeros((NU, 128), np.float32)
    base = N_META + core * 2048
    pos[0] = base - 128 + i
    for u in range(1, 17):
        pos[u] = base + (u - 1) * 128 + i
    pos[17] = PAST_LEN + (i % 8)
    inv = np.power(np.float32(10000.0), -np.arange(32, dtype=np.float32) * np.float32(2.0 / 64)).astype(np.float32)
    ang = (pos.reshape(-1, 1).astype(np.float32) * inv[None, :]).astype(np.float32)
    cosr = np.cos(ang.astype(np.float64)).astype(np.float32)
    sinr = np.sin(ang.astype(np.float64)).astype(np.float32)
    return masks, sh, cosr, sinr


_PROGRAM = None


def make_in_maps(x_prompt, x_sample, cache_k, cache_v, state_conv, meta_tokens, norm_w, w_in,
                 q_norm_w, k_norm_w, sinks, conv_w, w_proj_a, w_proj_b, w_out):
    f = lambda a: np.ascontiguousarray(np.asarray(a, dtype=np.float32))
    xp = f(x_prompt)[0]
    xs = f(x_sample)
    ck = f(cache_k)[0].reshape(128, 128, 256)
    cv = f(cache_v)[0].reshape(128, 128, 256)
    sc = f(state_conv)[0]
    meta = f(meta_tokens)
    shared = {"w_in": f(w_in)[0], "w_pa": f(w_proj_a)[0], "w_pb": f(w_proj_b)[0], "w_out": f(w_out)[0],
              "norm_w": f(norm_w)[0], "q_norm_w": f(q_norm_w)[0], "k_norm_w": f(k_norm_w)[0],
              "sinks": f(sinks)[0], "conv_w": f(conv_w)[0], "ident": np.eye(128, dtype=np.float32)}
    in_maps = []
    for c in range(NCORES):
        xin = np.zeros((NU * 128, D), np.float32)
        if c == 0:
            xin[128 - N_META:128] = meta
        else:
            xin[0:128] = xp[c * 2048 - 128:c * 2048]
        xin[128:128 + 2048] = xp[c * 2048:(c + 1) * 2048]
        xin[17 * 128:18 * 128] = xs[c * NSEQ:(c + 1) * NSEQ].reshape(128, D)
        masks, sh, cosr, sinr = _const_tables(c)
        m = dict(shared)
        m.update({"xin": xin, "cache_k": np.ascontiguousarray(ck[c * NSEQ:(c + 1) * NSEQ]),
                  "cache_v": np.ascontiguousarray(cv[c * NSEQ:(c + 1) * NSEQ]),
                  "state_conv": np.ascontiguousarray(sc[c * NSEQ:(c + 1) * NSEQ].reshape(NSEQ * 2, D)),
                  "cosr": cosr, "sinr": sinr, "masks": masks, "shifts": sh})
        in_maps.append(m)
    return in_maps


def kernel(**inputs):
    global _PROGRAM
    in_maps = make_in_maps(**inputs)
    if _PROGRAM is None:
        _PROGRAM = build_program()[0]
    res = run_bass_kernel_spmd(_PROGRAM, in_maps, core_ids=list(range(NCORES)))
    r = res.results
    y_prompt = np.concatenate([r[c]["y"][0:2048] for c in range(NCORES)], 0).reshape(1, 16384, D)
    y_sample = np.concatenate([r[c]["y"][2048:2176].reshape(NSEQ, 8, D) for c in range(NCORES)], 0)
    nk_p = r[NCORES - 1]["nk_p"].reshape(1, 1, 128, 4, 64)
    nv_p = r[NCORES - 1]["nv_p"].reshape(1, 1, 128, 4, 64)
    nc_p = r[NCORES - 1]["nc_p"].reshape(1, 1, 2, D)
    nk_s = np.concatenate([r[c]["nk_s"] for c in range(NCORES)], 0).reshape(1, 128, 128, 4, 64)
    nv_s = np.concatenate([r[c]["nv_s"] for c in range(NCORES)], 0).reshape(1, 128, 128, 4, 64)
    nc_s = np.concatenate([r[c]["nc_s"] for c in range(NCORES)], 0).reshape(1, 128, 2, D)
    return tuple(np.ascontiguousarray(a.astype(np.float32)) for a in
                 (y_prompt, y_sample, nk_p, nv_p, nc_p, nk_s, nv_s, nc_s))
```
